# Optimizing a Trainium2 kernel written in Bass

```python
import jax, jax.numpy as jnp
from jax import lax
import numpy as np

D_MODEL = 1024
BATCH = 16
SEQ = 2048
DEPTH = 4

CHUNK = 64
Q_BLOCK = 128
N_MEM = 256
FOX_HEADS = 8
FOX_HEAD_DIM = 64
MLA_HEADS = 8
MLA_NOPE_DIM = 64
MLA_ROPE_DIM = 32
MLA_V_DIM = 64
MLA_Q_RANK = 384
MLA_KV_RANK = 256
ROPE_BASE = 10000.0
MEM_HEADS = 4
MEM_HEAD_DIM = 128
N_BRANCHES = 3
BRANCH_WIDTH = 512
D_FF = 4 * D_MODEL
DEEPNORM_ALPHA = (2 * DEPTH) ** 0.25
DEEPNORM_BETA = (8 * DEPTH) ** -0.25
LN_EPS = 1e-5
RMS_EPS = 1e-6
NEG_INF = -1e30

IN_SPLITS = (
    3 * FOX_HEADS * FOX_HEAD_DIM,
    FOX_HEADS,
    MLA_Q_RANK,
    MLA_KV_RANK,
    MLA_ROPE_DIM,
    MEM_HEADS * MEM_HEAD_DIM,
    N_BRANCHES * D_MODEL,
)
D_IN = sum(IN_SPLITS)
SPLIT_POINTS = tuple(int(p) for p in np.cumsum(IN_SPLITS)[:-1])

kernel_name = 'hybrid_fox_mla_memory_deepnorm_trunk'


def _layer_norm(x, g, b):
    xf = x.astype(jnp.float32)
    mu = jnp.mean(xf, axis=-1, keepdims=True)
    var = jnp.mean(jnp.square(xf - mu), axis=-1, keepdims=True)
    y = (xf - mu) * lax.rsqrt(var + LN_EPS)
    return (y * g.astype(jnp.float32) + b.astype(jnp.float32)).astype(x.dtype)


def _rms_norm(x, g):
    xf = x.astype(jnp.float32)
    y = xf * lax.rsqrt(jnp.mean(jnp.square(xf), axis=-1, keepdims=True) + RMS_EPS)
    return (y * g.astype(jnp.float32)).astype(x.dtype)


def _rope_tables(positions):
    inv_freq = ROPE_BASE ** (-jnp.arange(0, MLA_ROPE_DIM, 2, dtype=jnp.float32) / MLA_ROPE_DIM)
    ang = positions.astype(jnp.float32)[..., None] * inv_freq
    return jnp.cos(ang), jnp.sin(ang)


def _rope(x, cos, sin):
    half = x.shape[-1] // 2
    x1 = x[..., :half].astype(jnp.float32)
    x2 = x[..., half:].astype(jnp.float32)
    return jnp.concatenate([x1 * cos - x2 * sin, x2 * cos + x1 * sin], axis=-1).astype(x.dtype)


def _swept_attention(q, k, v, bias_fn):
    seq = q.shape[2]
    scale = q.shape[-1] ** -0.5
    outs = []
    for i in range(seq // Q_BLOCK):
        q0 = i * Q_BLOCK
        k_len = q0 + Q_BLOCK
        logits = jnp.einsum('bhqd,bhkd->bhqk', q[:, :, q0:k_len], k[:, :, :k_len]).astype(jnp.float32)
        logits = logits * scale + bias_fn(q0, k_len)
        p = jax.nn.softmax(logits, axis=-1).astype(v.dtype)
        outs.append(jnp.einsum('bhqk,bhkd->bhqd', p, v[:, :, :k_len]))
    return jnp.concatenate(outs, axis=2)


def _chunk_causal_bias(q0, k_len):
    t_chunk = (q0 + jnp.arange(Q_BLOCK)) // CHUNK
    s_chunk = jnp.arange(k_len) // CHUNK
    allowed = s_chunk[None, :] <= t_chunk[:, None]
    return jnp.where(allowed, jnp.float32(0.0), jnp.float32(NEG_INF))[None, None]


def _mixer(h, mem, cos, sin, w_in, b_forget, w_uq, g_cq, w_ukv, g_ckv, w_mem_kv, w_br, w_out):
    B, S, _ = h.shape
    proj = h @ w_in
    fox_qkv, f_logit, c_q, c_kv, k_rope, q_mem, gate_logit = jnp.split(proj, SPLIT_POINTS, axis=-1)

    qkv = fox_qkv.reshape(B, S, 3, FOX_HEADS, FOX_HEAD_DIM).transpose(2, 0, 3, 1, 4)
    log_f = jax.nn.log_sigmoid(f_logit.astype(jnp.float32) + b_forget.astype(jnp.float32))
    cum_f = jnp.cumsum(log_f, axis=1).transpose(0, 2, 1)

    def fox_bias(q0, k_len):
        t = q0 + jnp.arange(Q_BLOCK)
        s = jnp.arange(k_len)
        decay = cum_f[:, :, q0:q0 + Q_BLOCK, None] - cum_f[:, :, None, :k_len]
        return jnp.where(s[None, :] <= t[:, None], decay, jnp.float32(NEG_INF))

    o_a = _swept_attention(qkv[0], qkv[1], qkv[2], fox_bias)
    o_a = o_a.transpose(0, 2, 1, 3).reshape(B, S, BRANCH_WIDTH)

    q_b = (_rms_norm(c_q, g_cq) @ w_uq).reshape(B, S, MLA_HEADS, MLA_NOPE_DIM + MLA_ROPE_DIM)
    q_b = q_b.transpose(0, 2, 1, 3)
    q_nope, q_pe = q_b[..., :MLA_NOPE_DIM], q_b[..., MLA_NOPE_DIM:]
    q_pe = _rope(q_pe, cos[:, None], sin[:, None])
    kv_b = (_rms_norm(c_kv, g_ckv) @ w_ukv).reshape(B, S, MLA_HEADS, MLA_NOPE_DIM + MLA_V_DIM)
    kv_b = kv_b.transpose(0, 2, 1, 3)
    k_nope, v_b = kv_b[..., :MLA_NOPE_DIM], kv_b[..., MLA_NOPE_DIM:]
    k_pe = _rope(k_rope, cos, sin)[:, None]
    q_full = jnp.concatenate([q_nope, q_pe], axis=-1)
    k_full = jnp.concatenate([k_nope, jnp.broadcast_to(k_pe, (B, MLA_HEADS, S, MLA_ROPE_DIM))], axis=-1)
    o_b = _swept_attention(q_full, k_full, v_b, _chunk_causal_bias)
    o_b = o_b.transpose(0, 2, 1, 3).reshape(B, S, BRANCH_WIDTH)

    mkv = (mem @ w_mem_kv).reshape(B, mem.shape[1], 2, MEM_HEADS, MEM_HEAD_DIM).transpose(2, 0, 3, 1, 4)
    qm = q_mem.reshape(B, S, MEM_HEADS, MEM_HEAD_DIM).transpose(0, 2, 1, 3)
    logits_m = jnp.einsum('bhqd,bhmd->bhqm', qm, mkv[0]).astype(jnp.float32) * (MEM_HEAD_DIM ** -0.5)
    p_m = jax.nn.softmax(logits_m, axis=-1).astype(mkv.dtype)
    o_c = jnp.einsum('bhqm,bhmd->bhqd', p_m, mkv[1]).transpose(0, 2, 1, 3).reshape(B, S, BRANCH_WIDTH)

    branches = jnp.stack([o_a, o_b, o_c], axis=2)
    branch_proj = jnp.einsum('bsnc,ncd->bsnd', branches, w_br)
    gates = jax.nn.sigmoid(gate_logit.reshape(B, S, N_BRANCHES, D_MODEL))
    merged = jnp.sum(gates * branch_proj, axis=2)
    return merged @ w_out


def setup_inputs(seed: int = 0) -> dict:
    key = jax.random.key(seed)
    ks = jax.random.split(key, 24)
    L = DEPTH

    def w(k, shape, fan_in, scale=1.0):
        return jax.random.normal(k, shape, jnp.float32) * (fan_in ** -0.5) * scale

    def gain(k, shape):
        return 1.0 + 0.02 * jax.random.normal(k, shape, jnp.float32)

    def bias(k, shape):
        return 0.02 * jax.random.normal(k, shape, jnp.float32)

    x = jax.random.normal(ks[0], (BATCH, SEQ, D_MODEL), jnp.float32)
    mem = jax.random.normal(ks[1], (BATCH, N_MEM, D_MODEL), jnp.float32)
    offsets = jax.random.randint(ks[2], (BATCH, 1), 0, 8192, dtype=jnp.int32)
    positions = offsets + jnp.arange(SEQ, dtype=jnp.int32)[None, :]
    return {
        'x': x,
        'mem': mem,
        'positions': positions,
        'ln_in_g': gain(ks[3], (D_MODEL,)),
        'ln_in_b': bias(ks[4], (D_MODEL,)),
        'w_in': w(ks[5], (L, D_MODEL, D_IN), D_MODEL),
        'b_forget': jax.random.uniform(ks[6], (L, FOX_HEADS), jnp.float32, 1.0, 6.0),
        'w_uq': w(ks[7], (L, MLA_Q_RANK, MLA_HEADS * (MLA_NOPE_DIM + MLA_ROPE_DIM)), MLA_Q_RANK),
        'g_cq': gain(ks[8], (L, MLA_Q_RANK)),
        'w_ukv': w(ks[9], (L, MLA_KV_RANK, MLA_HEADS * (MLA_NOPE_DIM + MLA_V_DIM)), MLA_KV_RANK),
        'g_ckv': gain(ks[10], (L, MLA_KV_RANK)),
        'w_mem_kv': w(ks[11], (L, D_MODEL, 2 * MEM_HEADS * MEM_HEAD_DIM), D_MODEL),
        'w_br': w(ks[12], (L, N_BRANCHES, BRANCH_WIDTH, D_MODEL), BRANCH_WIDTH, DEEPNORM_BETA),
        'w_out': w(ks[13], (L, D_MODEL, D_MODEL), D_MODEL, DEEPNORM_BETA),
        'ln1_g': gain(ks[14], (L, D_MODEL)),
        'ln1_b': bias(ks[15], (L, D_MODEL)),
        'w_ff1': w(ks[16], (L, D_MODEL, D_FF), D_MODEL),
        'w_ff2': w(ks[17], (L, D_FF, D_MODEL), D_FF, DEEPNORM_BETA),
        'ln2_g': gain(ks[18], (L, D_MODEL)),
        'ln2_b': bias(ks[19], (L, D_MODEL)),
    }


def reference(x, mem, positions, ln_in_g, ln_in_b, w_in, b_forget, w_uq, g_cq, w_ukv, g_ckv,
              w_mem_kv, w_br, w_out, ln1_g, ln1_b, w_ff1, w_ff2, ln2_g, ln2_b):
    cos, sin = _rope_tables(positions)
    h = _layer_norm(x, ln_in_g, ln_in_b)
    for l in range(DEPTH):
        y = _mixer(h, mem, cos, sin, w_in[l], b_forget[l], w_uq[l], g_cq[l], w_ukv[l], g_ckv[l],
                   w_mem_kv[l], w_br[l], w_out[l])
        h = _layer_norm(DEEPNORM_ALPHA * h + y, ln1_g[l], ln1_b[l])
        ff = jnp.square(jax.nn.relu(h @ w_ff1[l])) @ w_ff2[l]
        h = _layer_norm(DEEPNORM_ALPHA * h + ff, ln2_g[l], ln2_b[l])
    return h
```

```python
import numpy as np
import concourse.bass as bass
import concourse.mybir as mybir
from concourse.bass_utils import run_bass_kernel_spmd

F32 = mybir.dt.float32
BF16 = mybir.dt.bfloat16
I32 = mybir.dt.int32
AF = mybir.ActivationFunctionType
ALU = mybir.AluOpType

D = 1024
SEQ = 2048
DEPTH = 4
NCORE = 8
D_IN = 5800
C_FQ, C_FK, C_FV, C_F, C_CQ, C_CKV, C_KR, C_QM, C_G = 0, 512, 1024, 1536, 1544, 1928, 2184, 2216, 2728
ALPHA = float((2 * DEPTH) ** 0.25)
PI = float(np.pi)
TWO_PI = float(2 * np.pi)


class Res:
    __slots__ = ("name", "last_w", "readers", "dsem", "ring")

    def __init__(self, name):
        self.name = name
        self.last_w = None
        self.readers = []
        self.dsem = None
        self.ring = False


class DSem:
    __slots__ = ("sem", "count", "last", "ring")

    def __init__(self, sem):
        self.sem = sem
        self.count = 0
        self.last = None
        self.ring = False


class Op:
    __slots__ = ("eng", "fn", "deps", "idx", "need_inc", "cnt", "dsem", "dcount", "is_dma")

    def __init__(self, eng, fn, idx):
        self.eng = eng
        self.fn = fn
        self.idx = idx
        self.deps = []
        self.need_inc = False
        self.cnt = None
        self.dsem = None
        self.dcount = None
        self.is_dma = False


class Sched:
    ENGS = ("pe", "act", "dve", "pool", "sp")

    def __init__(self, nc):
        self.nc = nc
        self.eng_obj = {"pe": nc.tensor, "act": nc.scalar, "dve": nc.vector,
                        "pool": nc.gpsimd, "sp": nc.sync}
        self.ops = {e: [] for e in self.ENGS}
        self.sems = {}
        self._semctx = []
        for e in self.ENGS:
            self.sems[e] = self._new_sem("s_" + e)
        self.dsems = []

    def _new_sem(self, name):
        ctx = self.nc.semaphore(name)
        s = ctx.__enter__()
        self._semctx.append(ctx)
        return s

    def res(self, name):
        return Res(name)

    def dres(self, name, ring=False):
        r = Res(name)
        r.dsem = DSem(self._new_sem("d_%d" % len(self.dsems)))
        r.dsem.ring = ring
        self.dsems.append(r.dsem)
        return r

    def _add_dep(self, op, dep, raw):
        if dep is None or dep is op:
            return
        if dep.is_dma:
            op.deps.append(dep)
            return
        if dep.eng == op.eng:
            if op.is_dma:
                op.deps.append(dep)
                dep.need_inc = True
            elif raw and op.eng != "pe" and (op.idx - dep.idx) <= 2:
                op.deps.append(dep)
                dep.need_inc = True
            return
        op.deps.append(dep)
        dep.need_inc = True

    def op(self, eng, fn, reads=(), writes=(), dma=None):
        lst = self.ops[eng]
        o = Op(eng, fn, len(lst))
        if dma is not None:
            o.is_dma = True
            o.dsem = dma.dsem
            dma.dsem.count += 16
            o.dcount = dma.dsem.count
            dma.dsem.last = o
        for r in reads:
            self._add_dep(o, r.last_w, True)
        for w in writes:
            self._add_dep(o, w.last_w, False)
            for rd in w.readers:
                self._add_dep(o, rd, False)
        for r in reads:
            r.readers.append(o)
        for w in writes:
            w.last_w = o
            w.readers = []
        lst.append(o)
        return o

    def barrier(self):
        lasts = {}
        for e in self.ENGS:
            for o in reversed(self.ops[e]):
                if not o.is_dma and o.fn is not None:
                    lasts[e] = o
                    break
        dlast = [ds.last for ds in self.dsems if ds.last is not None and not ds.ring]
        for e in self.ENGS:
            lst = self.ops[e]
            o = Op(e, None, len(lst))
            for e2, lo in lasts.items():
                if e2 != e:
                    o.deps.append(lo)
                    lo.need_inc = True
            for d in dlast:
                o.deps.append(d)
            lst.append(o)

    def emit(self):
        for e in self.ENGS:
            c = 0
            for o in self.ops[e]:
                if o.is_dma or o.fn is None:
                    continue
                if o.need_inc:
                    c += 1
                    o.cnt = c
        for e in self.ENGS:
            eng = self.eng_obj[e]
            waited = {}
            for o in self.ops[e]:
                need = {}
                for d in o.deps:
                    if d.is_dma:
                        key, val = d.dsem.sem, d.dcount
                    else:
                        key, val = self.sems[d.eng], d.cnt
                    kid = id(key)
                    if waited.get(kid, 0) >= val:
                        continue
                    if kid not in need or need[kid][1] < val:
                        need[kid] = (key, val)
                for kid, (key, val) in need.items():
                    eng.wait_ge(key, val)
                    waited[kid] = val
                if o.fn is None:
                    continue
                ins = o.fn(eng)
                if o.is_dma:
                    ins.then_inc(o.dsem.sem, 16)
                elif o.need_inc:
                    ins.then_inc(self.sems[e], 1)
        sp = self.eng_obj["sp"]
        for ds in self.dsems:
            if ds.count > 0:
                sp.wait_ge(ds.sem, ds.count)

    def mm(self, out, lhsT, rhs, start, stop, reads, writes):
        return self.op("pe", lambda e: e.matmul(out, lhsT=lhsT, rhs=rhs, start=start, stop=stop), reads, writes)

    def actf(self, out, in_, func, reads, writes, bias=None, scale=1.0):
        if bias is None:
            return self.op("act", lambda e: e.activation(out=out, in_=in_, func=func, scale=scale), reads, writes)
        return self.op("act", lambda e: e.activation(out=out, in_=in_, func=func, bias=bias, scale=scale), reads, writes)

    def tt(self, eng, out, in0, in1, op, reads, writes):
        return self.op(eng, lambda e: e.tensor_tensor(out=out, in0=in0, in1=in1, op=op), reads, writes)

    def ts(self, eng, out, in0, s1, op0, reads, writes, s2=None, op1=None):
        if op1 is None:
            return self.op(eng, lambda e: e.tensor_scalar(out=out, in0=in0, scalar1=s1, scalar2=None, op0=op0), reads, writes)
        return self.op(eng, lambda e: e.tensor_scalar(out=out, in0=in0, scalar1=s1, scalar2=s2, op0=op0, op1=op1), reads, writes)

    def stt(self, eng, out, in0, scalar, in1, op0, op1, reads, writes):
        return self.op(eng, lambda e: e.scalar_tensor_tensor(out=out, in0=in0, scalar=scalar, in1=in1, op0=op0, op1=op1), reads, writes)

    def copy(self, eng, out, in_, reads, writes):
        if eng == "act":
            return self.op("act", lambda e: e.activation(out=out, in_=in_, func=AF.Copy), reads, writes)
        return self.op(eng, lambda e: e.tensor_copy(out=out, in_=in_), reads, writes)

    def recip(self, out, in_, reads, writes):
        return self.op("dve", lambda e: e.reciprocal(out=out, in_=in_), reads, writes)

    def memset(self, eng, ap, val, writes):
        return self.op(eng, lambda e: e.memset(ap, val), (), writes)

    def dma(self, eng, out, in_, track, reads=(), writes=()):
        return self.op(eng, lambda e: e.dma_start(out=out, in_=in_), reads, writes, dma=track)


def build_program(depth=DEPTH, nseq=2, debug=False):
    nc = bass.Bass("TRN2", target_bir_lowering=False)

    def din(name, shape, dt=F32):
        return nc.dram_tensor(name, list(shape), dt, kind="ExternalInput").ap()

    xT = din("xT", [nseq, D, SEQ])
    memT = din("memT", [nseq, D, 256])
    posr = din("posr", [nseq, 32, SEQ], I32)
    w_in = din("w_in", [DEPTH, D, D_IN])
    w_uq = din("w_uq", [DEPTH, 384, 768])
    w_ukv = din("w_ukv", [DEPTH, 256, 1024])
    w_mem_kv = din("w_mem_kv", [DEPTH, D, 1024])
    w_br = din("w_br", [DEPTH, 3, 512, D])
    w_out = din("w_out", [DEPTH, D, D])
    w_ff1 = din("w_ff1", [DEPTH, D, 4096])
    w_ff2 = din("w_ff2", [DEPTH, 4096, D])
    cvec = din("cvec", [128, 848])
    cmat = din("cmat", [128, 5, 128])
    outT = nc.dram_tensor("outT", [nseq, D, SEQ], F32, kind="ExternalOutput").ap()
    R0 = nc.dram_tensor("R0", [128, 8, SEQ], F32, kind="Internal").ap()
    R1 = nc.dram_tensor("R1", [128, 8, SEQ], F32, kind="Internal").ap()
    ROPE_D = nc.dram_tensor("ROPE_D", [32, 2, SEQ], F32, kind="Internal").ap()

    if debug:
        dbg_h0 = nc.dram_tensor("dbg_h0", [128, 8, SEQ], F32, kind="ExternalOutput").ap()
        dbg_h1 = nc.dram_tensor("dbg_h1", [128, 8, SEQ], F32, kind="ExternalOutput").ap()
        dbg_o = [nc.dram_tensor("dbg_o%d" % i, [128, 4, SEQ], BF16, kind="ExternalOutput").ap() for i in range(3)]
        dbg_mg = nc.dram_tensor("dbg_mg", [128, 8, SEQ], BF16, kind="ExternalOutput").ap()
    S = Sched(nc)
    ARENA_BYTES = 212000
    arena = nc.alloc_sbuf_tensor("arena", [128, ARENA_BYTES // 2], BF16)

    def vb(off, n):
        assert off % 4 == 0 and off + 2 * n <= ARENA_BYTES, (off, n)
        return arena[:, off // 2: off // 2 + n]

    def vf(off, n):
        assert off % 4 == 0 and off + 4 * n <= ARENA_BYTES, (off, n)
        return arena[:, off // 2: off // 2 + 2 * n].bitcast(F32)

    def vi(off, n):
        return arena[:, off // 2: off // 2 + 2 * n].bitcast(I32)

    o = 0
    HB = vb(o, 8 * SEQ).rearrange("p (c t) -> p c t", c=8); o += 32768
    CV = vf(o, 848); o += 3392
    CMF = vf(o, 3 * 128).rearrange("p (a b) -> p a b", a=3); o += 1536
    CMB = vb(o, 3 * 128).rearrange("p (a b) -> p a b", a=3); o += 768
    o = 38912
    RING = [vb(o + 8192 * i, 4096) for i in range(4)]; o += 32768
    WUQ = vb(o, 3 * 768).rearrange("p (k n) -> p k n", k=3); o += 4608
    WUQR = vb(o, 3 * 768).rearrange("p (k n) -> p k n", k=3); o += 4608
    WUKV = vb(o, 2 * 1024).rearrange("p (k n) -> p k n", k=2); o += 4096
    WKR = vb(o, 8 * 96).rearrange("p (k n) -> p k n", k=8); o += 1536
    WKRR = vb(o, 8 * 96).rearrange("p (k n) -> p k n", k=8); o += 1536
    PT = [vb(o + 1024 * i, 512) for i in range(6)]; o += 6144
    TMPF = [vf(o + 2048 * i, 512) for i in range(6)]; o += 12288
    XSQ = [vb(o + 1024 * i, 512) for i in range(4)]; o += 4096
    PR = o
    assert PR <= 110592, PR

    LNIN_G, LNIN_B = CV[:, 0:8], CV[:, 8:16]
    def LN1G(l): return CV[:, 16 + 8 * l: 24 + 8 * l]
    def LN1B(l): return CV[:, 48 + 8 * l: 56 + 8 * l]
    def LN2G(l): return CV[:, 80 + 8 * l: 88 + 8 * l]
    def LN2B(l): return CV[:, 112 + 8 * l: 120 + 8 * l]
    def GCQ(l): return CV[:, 144 + 3 * l: 147 + 3 * l]
    def GCKV(l): return CV[:, 156 + 2 * l: 158 + 2 * l]
    INVF = CV[:, 164:165]
    SGN = CV[:, 165:166]
    EPS5 = CV[:, 166:167]
    EPS6 = CV[:, 167:168]
    ONEC = CV[:, 168:169]
    ZEROC = CV[:, 169:170]
    def BFB(l): return CV[:, 176 + 128 * l: 304 + 128 * l]
    ONES_F, TRI_F, HALF_F = CMF[:, 0, :], CMF[:, 1, :], CMF[:, 2, :]
    ONESB, TRIMASK, MLAMASK = CMB[:, 0, :], CMB[:, 1, :], CMB[:, 2, :]

    r_hb = [S.res("hb%d" % t) for t in range(4)]
    r_const = S.dres("const")
    r_ring = [S.dres("ring%d" % i, ring=True) for i in range(4)]
    r_mlaw = S.dres("mlaw")
    r_pt = [S.res("pt%d" % i) for i in range(6)]
    r_tmp = [S.res("tmp%d" % i) for i in range(6)]
    r_xsq = [S.res("xsq%d" % i) for i in range(4)]
    r_ps = [S.res("ps%d" % i) for i in range(8)]
    PS = [nc.alloc_psum_tensor("ps%d" % i, [128, 512], F32)[:, :] for i in range(8)]
    r_xb = [[S.dres("xb%d_%d" % (t, c)) for c in range(8)] for t in range(3)]
    r_R0 = [S.res("R0_%d" % t) for t in range(4)]
    r_R1 = [S.res("R1_%d" % t) for t in range(4)]
    r_out = S.res("out")
    r_roped = S.res("roped")

    cnt = {"ps": 0, "pt": 0, "xsq": 0, "ring": 0, "acc": 0}

    def next_ps():
        i = cnt["ps"] % 5
        cnt["ps"] += 1
        return PS[i], r_ps[i]

    def next_acc():
        i = 5 + cnt["acc"] % 3
        cnt["acc"] += 1
        return PS[i], r_ps[i]

    def next_pt():
        i = cnt["pt"] % 6
        cnt["pt"] += 1
        return PT[i], r_pt[i]

    def next_xsq():
        i = cnt["xsq"] % 4
        cnt["xsq"] += 1
        return XSQ[i], r_xsq[i]

    ring_tasks = []
    ring_state = {"issued": 0, "used": 0}

    def ring_issue_upto(k):
        while ring_state["issued"] < min(k + 1, len(ring_tasks)):
            i = ring_state["issued"]
            slot = i % 4
            for dst, src in ring_tasks[i](RING[slot]):
                S.dma("pool", dst, src, r_ring[slot], writes=[r_ring[slot]])
            ring_state["issued"] += 1

    def ring_next():
        i = ring_state["used"]
        ring_issue_upto(i + 3)
        ring_state["used"] += 1
        slot = i % 4
        return RING[slot], r_ring[slot]

    def wsrc(w2d, c0, ncol):
        return w2d.rearrange("(k p) n -> p k n", p=128)[:, :, c0:c0 + ncol]

    for s in range(nseq):
        for l in range(depth):
            wl = w_in[l]
            for hp in range(4):
                def f(sl, wl=wl, hp=hp):
                    v = sl[:, 0:3072].rearrange("p (k n c) -> p k n c", k=8, n=3)
                    r = [(v[:, :, 0, :], wsrc(wl, C_FQ + hp * 128, 128)),
                         (v[:, :, 1, :], wsrc(wl, C_FK + hp * 128, 128)),
                         (v[:, :, 2, :], wsrc(wl, C_FV + hp * 128, 128))]
                    if hp == 0:
                        r.append((sl[:, 3072:3136].rearrange("p (k n) -> p k n", k=8), wsrc(wl, C_F, 8)))
                    return r
                ring_tasks.append(f)
            ring_tasks.append(lambda sl, wl=wl: [(sl[:, 0:2048].rearrange("p (k n) -> p k n", k=8), wsrc(wl, C_CKV, 256))])
            ring_tasks.append(lambda sl, wl=wl: [(sl[:, 0:3072].rearrange("p (k n) -> p k n", k=8), wsrc(wl, C_CQ, 384))])
            ring_tasks.append(lambda sl, l=l: [(sl[:, 0:4096].rearrange("p (k n) -> p k n", k=8), wsrc(w_mem_kv[l], 0, 512))])
            ring_tasks.append(lambda sl, l=l: [(sl[:, 0:4096].rearrange("p (k n) -> p k n", k=8), wsrc(w_mem_kv[l], 512, 512))])
            ring_tasks.append(lambda sl, wl=wl: [(sl[:, 0:4096].rearrange("p (k n) -> p k n", k=8), wsrc(wl, C_QM, 512))])
            for c in range(8):
                def fg(sl, wl=wl, c=c):
                    v = sl[:, 0:3072].rearrange("p (k n c) -> p k n c", k=8, n=3)
                    return [(v[:, :, n, :], wsrc(wl, C_G + n * 1024 + c * 128, 128)) for n in range(3)]
                ring_tasks.append(fg)
            for gt in range(4):
                for ch in range(2):
                    ring_tasks.append(lambda sl, l=l, ch=ch: [(sl[:, 0:4096].rearrange("p (k n) -> p k n", k=8), wsrc(w_out[l], ch * 512, 512))])
            for hf in range(2):
                for g in range(8):
                    ring_tasks.append(lambda sl, l=l, g=g: [(sl[:, 0:4096].rearrange("p (k n) -> p k n", k=8), wsrc(w_ff1[l], g * 512, 512))])
                for c in range(8):
                    ring_tasks.append(lambda sl, l=l, c=c: [(sl[:, 0:4096].rearrange("p (k n) -> p k n", k=32), wsrc(w_ff2[l], c * 128, 128))])

    S.dma("sp", CV, cvec, r_const, writes=[r_const])
    S.dma("sp", CMF, cmat[:, 0:3, :], r_const, writes=[r_const])
    S.dma("pool", CMB[:, 0, :], cmat[:, 0, :], r_const, writes=[r_const])
    S.dma("pool", CMB[:, 1:3, :], cmat[:, 3:5, :], r_const, writes=[r_const])
    S.memset("pool", WUQR, 0.0, [r_mlaw])
    S.memset("pool", WKR, 0.0, [r_mlaw])
    S.memset("pool", WKRR, 0.0, [r_mlaw])

    XB = vf(PR, 8 * 1024).rearrange("p (c t) -> p c t", c=8)

    def ln_stats(X, rX, k):
        psM, rM = next_acc()
        psV, rV = next_acc()
        for c in range(8):
            xb, rxb = next_xsq()
            S.copy("act", xb, X[:, c, :], [rX[c]], [rxb])
            sq, rsq = next_xsq()
            S.actf(sq, X[:, c, :], AF.Square, [rX[c]], [rsq])
            S.mm(psM, ONESB, xb, c == 0, c == 7, [rxb, r_const], [rM])
            S.mm(psV, ONESB, sq, c == 0, c == 7, [rsq, r_const], [rV])
        MEAN, RSTD = TMPF[2 * k], TMPF[2 * k + 1]
        rmean, rrstd = r_tmp[2 * k], r_tmp[2 * k + 1]
        S.actf(MEAN, psM, AF.Copy, [rM], [rmean], scale=1.0 / 1024)
        S.tt("dve", RSTD, MEAN, MEAN, ALU.mult, [rmean], [rrstd])
        S.stt("dve", RSTD, psV, 1.0 / 1024, RSTD, ALU.mult, ALU.subtract, [rV, rrstd], [rrstd])
        S.actf(RSTD, RSTD, AF.Ln, [rrstd, r_const], [rrstd], bias=EPS5)
        S.actf(RSTD, RSTD, AF.Exp, [rrstd], [rrstd], scale=-0.5)

    def ln_apply(X, rX, k, gtile, G, Bv):
        gcols = slice(gtile * 512, gtile * 512 + 512)
        MEAN, RSTD = TMPF[2 * k], TMPF[2 * k + 1]
        rmean, rrstd = r_tmp[2 * k], r_tmp[2 * k + 1]
        for c in range(8):
            x = X[:, c, :]
            rx = rX[c]
            S.tt("dve", x, x, MEAN, ALU.subtract, [rx, rmean], [rx])
            S.stt("dve", x, x, G[:, c:c + 1], RSTD, ALU.mult, ALU.mult, [rx, rrstd, r_const], [rx])
            S.actf(HB[:, c, gcols], x, AF.Identity, [rx, r_const], [r_hb[gtile]], bias=Bv[:, c:c + 1])
            S.actf(x, x, AF.Identity, [rx, r_const], [rx], bias=Bv[:, c:c + 1])

    def layer_norm_x(X, rX, gtile, G, Bv):
        ln_stats(X, rX, 0)
        ln_apply(X, rX, 0, gtile, G, Bv)

    def XBT(tl):
        return XB[:, :, tl * 512: tl * 512 + 512]

    def layer_norm_tile(tl, gtile, G, Bv):
        layer_norm_x(XBT(tl), r_xb[tl], gtile, G, Bv)

    def store_x(X, rX, dst, rdst):
        return S.dma("sp", dst, X, rX[0], reads=list(rX), writes=[rdst])

    def load_x(X, rX, src, rsrc):
        S.dma("sp", X, src, rX[0], reads=[rsrc], writes=list(rX))

    def store_xb_tile(tl, dst, rdst):
        store_x(XBT(tl), r_xb[tl], dst, rdst)

    def load_xb_half(src, rsrcs):
        for tl in range(2):
            load_x(XBT(tl), r_xb[tl], src[:, :, tl * 512: tl * 512 + 512], rsrcs[tl])

    def run_units(nunits, make_unit, on_start, L=3):
        units = {}
        njs = {}
        steps = []
        live = {}
        total = None
        g = 0
        pending = []
        ui_next = 0
        while True:
            while len(pending) <= g and ui_next < nunits:
                u = make_unit(ui_next)
                units[ui_next] = u
                for j in range(u[0]):
                    pending.append((ui_next, j))
                ui_next += 1
            if g < len(pending):
                ui, j = pending[g]
                if j == 0:
                    on_start(ui)
                live[g] = units[ui][1](j)
            if g >= L and g - L < len(pending):
                ui, j = pending[g - L]
                units[ui][2](j, *live.pop(g - L))
                if j == units[ui][0] - 1:
                    units[ui][3]()
            g += 1
            if ui_next >= nunits and g - L >= len(pending):
                break

    def pipelined(nj, s_fn, pv_fn, L=3):
        live = {}
        for step in range(nj + L):
            if step < nj:
                live[step] = s_fn(step)
            if step >= L:
                pv_fn(step - L, *live.pop(step - L))

    def R_tile(R, gt):
        return R[:, :, gt * 512: gt * 512 + 512]

    for s in range(nseq):
        xs = xT[s].rearrange("(c p) t -> p c t", p=128)
        os_ = outT[s].rearrange("(c p) t -> p c t", p=128)
        r_x = S.res("xin")
        S.barrier()
        if True:
            o_alias = PR + 49152
            COS = vf(PR + 32768, SEQ)
            SINP = vf(PR + 32768 + 8192, SEQ)
            PI_ = vi(o_alias, SEQ)
            ANG = vf(o_alias + 8192, SEQ)
            KFL = vf(o_alias + 16384, SEQ)
            r_rope = S.dres("rope")
            r_rt = S.res("ropetmp")
            rp_ = slice(64, 96)
            S.dma("sp", PI_[rp_, :], posr[s], r_rope, writes=[r_rt])
            S.copy("dve", ANG[rp_, :], PI_[rp_, :], [r_rt], [r_rt])
            S.ts("dve", ANG[rp_, :], ANG[rp_, :], INVF[rp_, :], ALU.mult, [r_rt, r_const], [r_rt])
            for tab, shift in ((SINP, 0.0), (COS, PI / 2)):
                A2 = tab
                S.ts("dve", A2[rp_, :], ANG[rp_, :], shift, ALU.add, [r_rt], [r_rope])
                S.ts("dve", KFL[rp_, :], A2[rp_, :], 1.0 / TWO_PI, ALU.mult, [r_rope], [r_rt])
                S.copy("dve", PI_[rp_, :], KFL[rp_, :], [r_rt], [r_rt])
                S.copy("dve", KFL[rp_, :], PI_[rp_, :], [r_rt], [r_rt])
                S.stt("dve", A2[rp_, :], KFL[rp_, :], -TWO_PI, A2[rp_, :], ALU.mult, ALU.add, [r_rt, r_rope], [r_rope])
                S.ts("dve", KFL[rp_, :], A2[rp_, :], PI, ALU.is_gt, [r_rope], [r_rt], s2=-TWO_PI, op1=ALU.mult)
                S.tt("dve", A2[rp_, :], A2[rp_, :], KFL[rp_, :], ALU.add, [r_rope, r_rt], [r_rope])
                S.ts("dve", KFL[rp_, :], A2[rp_, :], -PI, ALU.is_lt, [r_rope], [r_rt], s2=TWO_PI, op1=ALU.mult)
                S.tt("dve", A2[rp_, :], A2[rp_, :], KFL[rp_, :], ALU.add, [r_rope, r_rt], [r_rope])
                S.actf(A2[rp_, :], A2[rp_, :], AF.Sin, [r_rope], [r_rope])
            S.ts("dve", SINP[rp_, :], SINP[rp_, :], SGN[rp_, :], ALU.mult, [r_rope, r_const], [r_rope])
            S.dma("sp", ROPE_D[:, 0, :], COS[rp_, :], r_rope, reads=[r_rope], writes=[r_roped])
            S.dma("sp", ROPE_D[:, 1, :], SINP[rp_, :], r_rope, reads=[r_rope], writes=[r_roped])
        S.barrier()
        for hf in range(2):
            load_xb_half(xs[:, :, hf * 1024:(hf + 1) * 1024], [r_x, r_x])
            for tl in range(2):
                gt = hf * 2 + tl
                layer_norm_tile(tl, gt, LNIN_G, LNIN_B)
                store_xb_tile(tl, R_tile(R0, gt), r_R0[gt])
                if debug and s == 0:
                    store_xb_tile(tl, R_tile(dbg_h0, gt), S.res("dbg"))

        for l in range(depth):
            wl = w_in[l]
            if l == 0:
                S.barrier()
                prev_tail = []
            S.dma("pool", WUQ, w_uq[l].rearrange("(k p) n -> p k n", p=128), r_mlaw, writes=[r_mlaw])
            S.dma("pool", WUKV, w_ukv[l].rearrange("(k p) n -> p k n", p=128), r_mlaw, writes=[r_mlaw])
            uq4 = w_uq[l].rearrange("(k p) (h c) -> p k h c", p=128, c=96)
            wr4 = WUQR.rearrange("p k (h c) -> p k h c", c=96)
            for k in range(3):
                S.dma("pool", wr4[:, k, :, 64:80], uq4[:, k, :, 80:96], r_mlaw, writes=[r_mlaw])
                S.dma("pool", wr4[:, k, :, 80:96], uq4[:, k, :, 64:80], r_mlaw, writes=[r_mlaw])
            S.dma("pool", WKR[:, :, 64:96], wsrc(wl, C_KR, 32), r_mlaw, writes=[r_mlaw])
            S.dma("pool", WKRR[:, :, 64:80], wsrc(wl, C_KR + 16, 16), r_mlaw, writes=[r_mlaw])
            S.dma("pool", WKRR[:, :, 80:96], wsrc(wl, C_KR, 16), r_mlaw, writes=[r_mlaw])

            OA = vb(PR, 4 * SEQ).rearrange("p (c t) -> p c t", c=4)
            OB = vb(PR + 16384, 4 * SEQ).rearrange("p (c t) -> p c t", c=4)
            OC = vb(PR + 32768, 4 * SEQ).rearrange("p (c t) -> p c t", c=4)
            r_oa = [S.res("oa%d" % t) for t in range(4)]
            for t in range(4):
                r_oa[t].readers.extend(prev_tail)
            r_ob = [S.res("ob%d" % t) for t in range(4)]
            r_oc = [S.res("oc%d" % t) for t in range(4)]
            P2 = PR + 49152

            o = P2
            BIAS = vf(o, 136 * 8); o += 4608
            GT = vf(o, 128); o += 512
            PRE = vf(o, 128); o += 512
            CC = vf(o, 128); o += 512
            CMID = vf(o, 128); o += 512
            KT = [vb(o + 4096 * i, SEQ) for i in range(2)]; o += 8192
            VA = [vb(o + 8192 * i, 16 * 256).rearrange("p (t c) -> p t c", t=16) for i in range(2)]; o += 16384
            QTZ = [[vb(o + 2048 * i + 1024 * e, 512) for e in range(2)] for i in range(2)]; o += 4096
            RS2 = [vf(o + 2048 * i, 512) for i in range(2)]; o += 4096
            assert o <= ARENA_BYTES
            r_bias = [S.res("bias%d" % t) for t in range(4)]; r_g = S.res("g"); r_pre = S.res("pre"); r_cc = S.res("cc")
            r_kt = [S.res("kt0"), S.res("kt1")]
            r_va = [S.res("va0"), S.res("va1")]
            r_qt = [[S.res("qt%d%d" % (i, e)) for e in range(2)] for i in range(2)]
            r_rs = [S.res("rs0"), S.res("rs1")]
            for i in range(2):
                va4 = VA[i].rearrange("p t (a c) -> p t a c", a=4)
                S.memset("pool", va4[:, :, 1, :], 1.0, [r_va[i]])
                S.memset("pool", va4[:, :, 3, :], 1.0, [r_va[i]])
                S.memset("pool", QTZ[i][0][64:128, :], 0.0, [r_qt[i][0]])
                S.memset("pool", QTZ[i][1][0:64, :], 0.0, [r_qt[i][1]])

            fox_state = {}

            def fox_prep_pair(hp):
                sl, rsl = ring_next()
                wv3 = sl[:, 0:3072].rearrange("p (k n c) -> p k n c", k=8, n=3)
                fox_state[hp] = (wv3, rsl)
                if hp == 0:
                    wf = sl[:, 3072:3136].rearrange("p (k n) -> p k n", k=8)
                    psF, rF = next_acc()
                    for t16 in range(16):
                        for kc in range(8):
                            S.mm(psF[:, t16 * 8: t16 * 8 + 8], HB[:, kc, t16 * 128: t16 * 128 + 128], wf[:, kc, :],
                                 kc == 0, kc == 7, [r_hb[t16 // 4], rsl], [rF])
                    S.tt("dve", GT, psF[:, 0:128], BFB(l), ALU.add, [rF, r_const], [r_g])
                    S.actf(GT, GT, AF.Exp, [r_g], [r_g], scale=-1.0)
                    S.actf(GT, GT, AF.Ln, [r_g, r_const], [r_g], bias=ONEC)
                b = hp % 2
                for t in range(4):
                    ps, rp = next_ps()
                    for kc in range(8):
                        S.mm(ps, wv3[:, kc, 1, :], HB[:, kc, t * 512: t * 512 + 512], kc == 0, kc == 7, [r_hb[t], rsl], [rp])
                    S.copy("dve", KT[b][:, t * 512: t * 512 + 512], ps, [rp], [r_kt[b]])
                va4 = VA[b].rearrange("p t (a c) -> p t a c", a=4)
                for g4 in range(4):
                    ps, rp = next_ps()
                    for tt_ in range(4):
                        t16 = g4 * 4 + tt_
                        for kc in range(8):
                            S.mm(ps[:, tt_ * 128: tt_ * 128 + 128], HB[:, kc, t16 * 128: t16 * 128 + 128], wv3[:, kc, 2, :],
                                 kc == 0, kc == 7, [r_hb[g4], rsl], [rp])
                    ps3 = ps.rearrange("p (t c) -> p t c", t=4)
                    S.copy("dve", va4[:, g4 * 4: g4 * 4 + 4, 0, :], ps3[:, :, 0:64], [rp], [r_va[b]])
                    S.copy("dve", va4[:, g4 * 4: g4 * 4 + 4, 2, :], ps3[:, :, 64:128], [rp], [r_va[b]])

                if hp == 0:
                    psT, rT = next_acc()
                    psW, rW = next_acc()
                    psMd, rMd = next_acc()
                    S.mm(psT[:, 0:128], ONES_F, GT, True, True, [r_g, r_const], [rT])
                    S.mm(psW[:, 0:128], TRI_F, GT, True, True, [r_g, r_const], [rW])
                    S.mm(psMd[:, 0:128], HALF_F, GT, True, True, [r_g, r_const], [rMd])
                    S.memset("dve", PRE[:, 0:8], 0.0, [r_pre])
                    for j in range(1, 16):
                        S.tt("dve", PRE[:, j * 8: j * 8 + 8], PRE[:, j * 8 - 8: j * 8], psT[:, j * 8 - 8: j * 8], ALU.add,
                             [r_pre, rT], [r_pre])
                    S.tt("dve", CC, PRE, psW[:, 0:128], ALU.add, [r_pre, rW], [r_cc])
                    S.tt("dve", CMID, PRE, psMd[:, 0:128], ALU.add, [r_pre, rMd], [r_cc])
                    for i in range(16):
                        for j in range(i + 1):
                            idx = i * (i + 1) // 2 + j
                            S.tt("pool", BIAS[:, idx * 8: idx * 8 + 8], CC[:, j * 8: j * 8 + 8], CMID[:, i * 8: i * 8 + 8],
                                 ALU.subtract, [r_cc], [r_bias[i // 4]])

            def fox_prep_q(hp, t, qb):
                wv3, rsl = fox_state[hp]
                ps, rp = next_ps()
                for kc in range(8):
                    S.mm(ps, wv3[:, kc, 0, :], HB[:, kc, t * 512: t * 512 + 512], kc == 0, kc == 7, [r_hb[t], rsl], [rp])
                S.copy("act", QTZ[qb][0][0:64, :], ps[0:64, :], [rp], [r_qt[qb][0]])
                S.copy("act", QTZ[qb][1][64:128, :], ps[64:128, :], [rp], [r_qt[qb][1]])

            def fox_unit(hp, t, e, qb, ri):
                b = hp % 2
                h = 2 * hp + e
                prt = slice(e * 64, e * 64 + 64)
                nj = 4 * t + 4
                st = {}

                def s_fn(j):
                    if j == 0:
                        st["psO"], st["rO"] = next_acc()
                    d = j - 4 * t
                    c0 = max(0, d) * 128
                    psS, rS_ = next_ps()
                    S.mm(psS[:, c0:512], KT[b][:, j * 128: j * 128 + 128], QTZ[qb][e][:, c0:512], True, True,
                         [r_kt[b], r_qt[qb][e]], [rS_])
                    pt, rpt = next_pt()
                    for qi in range(c0 // 128, 4):
                        i = 4 * t + qi
                        idx = i * (i + 1) // 2 + j
                        S.actf(pt[:, qi * 128: qi * 128 + 128], psS[:, qi * 128: qi * 128 + 128], AF.Exp,
                               [rS_, r_bias[t]], [rpt], bias=BIAS[:, idx * 8 + h: idx * 8 + h + 1], scale=0.125)
                    if d >= 0:
                        S.tt("dve", pt[:, c0:c0 + 128], pt[:, c0:c0 + 128], TRIMASK, ALU.mult, [rpt, r_const], [rpt])
                    return pt, rpt, c0

                def pv_fn(j, pt, rpt, c0):
                    S.mm(st["psO"][:, c0:512], VA[b][:, j, e * 128: e * 128 + 128], pt[:, c0:512], j == 0, j == nj - 1,
                         [r_va[b], rpt], [st["rO"]])

                def fin_fn():
                    psO, rO = st["psO"], st["rO"]
                    RSx, rrs = RS2[ri], r_rs[ri]
                    S.recip(RSx[0:64, :], psO[64:128, :], [rO], [rrs])
                    S.tt("dve", OA[prt, hp, t * 512: t * 512 + 512], psO[0:64, :], RSx[0:64, :], ALU.mult,
                         [rO, rrs], [r_oa[t]])
                return nj, s_fn, pv_fn, fin_fn

            pairs = [(hp, t) for hp in range(4) for t in range(4)]
            fox_prep_pair(0)
            fox_prep_q(0, 0, 0)

            def fox_make(ui):
                hp, t = pairs[ui // 2]
                return fox_unit(hp, t, ui % 2, (ui // 2) % 2, ui % 2)

            def fox_start(ui):
                if ui % 2 == 1 and ui // 2 + 1 < len(pairs):
                    hp, t = pairs[ui // 2]
                    nhp, nt = pairs[ui // 2 + 1]
                    if nhp != hp:
                        fox_prep_pair(nhp)
                    fox_prep_q(nhp, nt, (ui // 2 + 1) % 2)
            run_units(2 * len(pairs), fox_make, fox_start)

            S.barrier()
            o = P2
            COS = vf(PR + 32768, SEQ)
            SINP = vf(PR + 32768 + 8192, SEQ)
            CKV = vb(o, 2 * SEQ).rearrange("p (c t) -> p c t", c=2); o += 8192
            KPE = vb(o, SEQ); o += 4096
            RAW = [vf(o + 2048 * i, 512) for i in range(3)]; o += 6144
            RINV = vf(o, 512); o += 2048
            QF = [vb(o + 1024 * i, 512) for i in range(2)]; o += 2048
            o_alias = o
            CQ = vb(o, 3 * SEQ).rearrange("p (c t) -> p c t", c=3); o += 12288
            KF = [vb(o + 4096 * i, SEQ) for i in range(2)]; o += 8192
            VAH = [vb(o + 4096 * i, 16 * 128).rearrange("p (t c) -> p t c", t=16) for i in range(2)]; o += 8192
            assert o <= ARENA_BYTES, o
            r_rope = S.dres("rope")
            rp_ = slice(64, 96)
            S.dma("sp", COS[rp_, :], ROPE_D[:, 0, :], r_rope, reads=[r_roped], writes=[r_rope])
            S.dma("sp", SINP[rp_, :], ROPE_D[:, 1, :], r_rope, reads=[r_roped], writes=[r_rope])

            r_ckv = [S.res("ckv%d" % t) for t in range(4)]
            r_cq = [S.res("cq%d" % t) for t in range(4)]
            r_kpe = S.res("kpe")
            r_raw = [S.res("raw%d" % i) for i in range(3)]
            r_rinv = S.res("rinv")
            r_rs = S.res("rs2")
            r_qf = [S.res("qf0"), S.res("qf1")]
            r_kf = [S.res("kf0"), S.res("kf1")]
            r_vah = [S.res("vah0"), S.res("vah1")]
            for i in range(2):
                S.memset("pool", VAH[i][:, :, 64:128], 1.0, [r_vah[i]])

            def latent(sl, rsl, nch, dst, rdst, gfn, t, inv_n):
                psN, rN = next_acc()
                for cch in range(nch):
                    ps, rp = next_ps()
                    for kc in range(8):
                        S.mm(ps, sl[:, kc, cch * 128: cch * 128 + 128], HB[:, kc, t * 512: t * 512 + 512], kc == 0, kc == 7,
                             [r_hb[t], rsl], [rp])
                    S.copy("act", RAW[cch], ps, [rp], [r_raw[cch]])
                    sq, rsq = next_xsq()
                    S.actf(sq, ps, AF.Square, [rp], [rsq])
                    S.mm(psN, ONESB, sq, cch == 0, cch == nch - 1, [rsq, r_const], [rN])
                S.actf(RINV, psN, AF.Ln, [rN, r_const], [r_rinv], bias=EPS6, scale=inv_n)
                S.actf(RINV, RINV, AF.Exp, [r_rinv], [r_rinv], scale=-0.5)
                for cch in range(nch):
                    S.stt("dve", dst[:, cch, t * 512: t * 512 + 512], RAW[cch], gfn[:, cch:cch + 1], RINV, ALU.mult, ALU.mult,
                          [r_raw[cch], r_rinv, r_const], [rdst[t]])

            sl, rsl = ring_next()
            wckv = sl[:, 0:2048].rearrange("p (k n) -> p k n", k=8)
            for t in range(4):
                latent(wckv, rsl, 2, CKV, r_ckv, GCKV(l), t, 1.0 / 256)
                psA, rA = next_ps()
                psB, rB = next_ps()
                for kc in range(8):
                    S.mm(psA[0:96, :], WKR[:, kc, :], HB[:, kc, t * 512: t * 512 + 512], kc == 0, kc == 7, [r_hb[t], r_mlaw], [rA])
                for kc in range(8):
                    S.mm(psB[0:96, :], WKRR[:, kc, :], HB[:, kc, t * 512: t * 512 + 512], kc == 0, kc == 7, [r_hb[t], r_mlaw], [rB])
                tc_ = slice(t * 512, t * 512 + 512)
                S.tt("dve", TMPF[2][rp_, :], psA[rp_, :], COS[rp_, tc_], ALU.mult, [rA, r_rope], [r_tmp[2]])
                S.tt("dve", TMPF[3][rp_, :], psB[rp_, :], SINP[rp_, tc_], ALU.mult, [rB, r_rope], [r_tmp[3]])
                S.tt("pool", KPE[rp_, tc_], TMPF[2][rp_, :], TMPF[3][rp_, :], ALU.add, [r_tmp[2], r_tmp[3]], [r_kpe])
            sl, rsl = ring_next()
            wcq = sl[:, 0:3072].rearrange("p (k n) -> p k n", k=8)
            for t in range(4):
                latent(wcq, rsl, 3, CQ, r_cq, GCQ(l), t, 1.0 / 384)

            SC_MLA = float(96 ** -0.5)
            for i in range(2):
                S.memset("pool", KF[i], 0.0, [r_kf[i]])
                S.memset("pool", QF[i], 0.0, [r_qf[i]])
            RS2 = [RAW[1], RAW[2]]
            r_rs2 = [r_raw[1], r_raw[2]]

            def mla_prep_head(h):
                b = h % 2
                for t in range(4):
                    ps, rp = next_ps()
                    for kc in range(2):
                        S.mm(ps[0:64, :], WUKV[:, kc, h * 128: h * 128 + 64], CKV[:, kc, t * 512: t * 512 + 512], kc == 0, kc == 1,
                             [r_ckv[t], r_mlaw], [rp])
                    S.copy("dve", KF[b][0:64, t * 512: t * 512 + 512], ps[0:64, :], [rp], [r_kf[b]])
                S.copy("pool", KF[b][rp_, :], KPE[rp_, :], [r_kpe], [r_kf[b]])
                for g8 in range(2):
                    ps, rp = next_ps()
                    for tt_ in range(8):
                        t16 = g8 * 8 + tt_
                        for kc in range(2):
                            S.mm(ps[:, tt_ * 64: tt_ * 64 + 64], CKV[:, kc, t16 * 128: t16 * 128 + 128],
                                 WUKV[:, kc, h * 128 + 64: h * 128 + 128], kc == 0, kc == 1, [r_ckv[t16 // 4], r_mlaw], [rp])
                    S.copy("dve", VAH[b][:, g8 * 8: g8 * 8 + 8, 0:64], ps.rearrange("p (t c) -> p t c", t=8), [rp], [r_vah[b]])

            def mla_prep_q(h, t, qb):
                tc_ = slice(t * 512, t * 512 + 512)
                psA, rA = next_ps()
                psB, rB = next_ps()
                for kc in range(3):
                    S.mm(psA[0:96, :], WUQ[:, kc, h * 96: h * 96 + 96], CQ[:, kc, tc_], kc == 0, kc == 2, [r_cq[t], r_mlaw], [rA])
                for kc in range(3):
                    S.mm(psB[0:96, :], WUQR[:, kc, h * 96: h * 96 + 96], CQ[:, kc, tc_], kc == 0, kc == 2, [r_cq[t], r_mlaw], [rB])
                S.copy("dve", QF[qb][0:64, :], psA[0:64, :], [rA], [r_qf[qb]])
                ta, tb = (2, 3) if qb == 0 else (4, 5)
                S.tt("dve", TMPF[ta][rp_, :], psA[rp_, :], COS[rp_, tc_], ALU.mult, [rA, r_rope], [r_tmp[ta]])
                S.tt("dve", TMPF[tb][rp_, :], psB[rp_, :], SINP[rp_, tc_], ALU.mult, [rB, r_rope], [r_tmp[tb]])
                S.tt("pool", QF[qb][rp_, :], TMPF[ta][rp_, :], TMPF[tb][rp_, :], ALU.add, [r_tmp[ta], r_tmp[tb]], [r_qf[qb]])

            def mla_unit(h, t, qb):
                b = h % 2
                tc_ = slice(t * 512, t * 512 + 512)
                nj = 4 * t + 4
                st = {}

                def s_fn(j):
                    if j == 0:
                        st["psO"], st["rO"] = next_acc()
                    d = j - 4 * t
                    c0 = max(0, d) * 128
                    psS, rS_ = next_ps()
                    S.mm(psS[:, c0:512], KF[b][:, j * 128: j * 128 + 128], QF[qb][:, c0:512], True, True,
                         [r_kf[b], r_qf[qb]], [rS_])
                    pt, rpt = next_pt()
                    S.actf(pt[:, c0:512], psS[:, c0:512], AF.Exp, [rS_], [rpt], scale=SC_MLA)
                    if d >= 0:
                        S.tt("dve", pt[:, c0:c0 + 128], pt[:, c0:c0 + 128], MLAMASK, ALU.mult, [rpt, r_const], [rpt])
                    return pt, rpt, c0

                def pv_fn(j, pt, rpt, c0):
                    S.mm(st["psO"][:, c0:512], VAH[b][:, j, :], pt[:, c0:512], j == 0, j == nj - 1, [r_vah[b], rpt], [st["rO"]])

                def fin_fn():
                    psO, rO = st["psO"], st["rO"]
                    RSx, rrs = RS2[qb], r_rs2[qb]
                    S.actf(RSx[0:64, :], psO[64:128, :], AF.Ln, [rO], [rrs])
                    S.actf(RSx[0:64, :], RSx[0:64, :], AF.Exp, [rrs], [rrs], scale=-1.0)
                    S.tt("dve", OB[(h % 2) * 64:(h % 2) * 64 + 64, h // 2, tc_], psO[0:64, :], RSx[0:64, :], ALU.mult,
                         [rO, rrs], [r_ob[t]])
                return nj, s_fn, pv_fn, fin_fn

            munits = [(h, t) for h in range(8) for t in range(4)]
            mla_prep_head(0)
            mla_prep_q(0, 0, 0)

            def mla_make(ui):
                h, t = munits[ui]
                return mla_unit(h, t, ui % 2)

            def mla_start(ui):
                if ui + 1 < len(munits):
                    h, t = munits[ui]
                    nh, nt = munits[ui + 1]
                    if nh != h:
                        mla_prep_head(nh)
                    mla_prep_q(nh, nt, (ui + 1) % 2)
            run_units(len(munits), mla_make, mla_start)

            S.barrier()
            o = P2
            MEMT = vb(o, 8 * 256).rearrange("p (k m) -> p k m", k=8); o += 4096
            KM = vb(o, 4 * 256).rearrange("p (h m) -> p h m", h=4); o += 2048
            VM = vb(o, 2 * 512).rearrange("p (c n) -> p c n", c=2); o += 2048
            QM = [vb(o + 1024 * i, 512) for i in range(2)]; o += 2048
            RIM = vf(o, 512); o += 2048
            r_memt = S.dres("memt")
            r_km = S.res("km"); r_vm = S.res("vm")
            r_qm = [S.res("qm0"), S.res("qm1")]
            r_rim = S.res("rim")
            S.dma("pool", MEMT, memT[s].rearrange("(k p) m -> p k m", p=128), r_memt, writes=[r_memt])
            sl, rsl = ring_next()
            wmk = sl[:, 0:4096].rearrange("p (k n) -> p k n", k=8)
            for h in range(4):
                ps, rp = next_ps()
                for kc in range(8):
                    S.mm(ps[:, 0:256], wmk[:, kc, h * 128: h * 128 + 128], MEMT[:, kc, :], kc == 0, kc == 7, [r_memt, rsl], [rp])
                S.copy("act", KM[:, h, :], ps[:, 0:256], [rp], [r_km])
            sl, rsl = ring_next()
            wmv = sl[:, 0:4096].rearrange("p (k n) -> p k n", k=8)
            for mc in range(2):
                ps, rp = next_ps()
                for kc in range(8):
                    S.mm(ps, MEMT[:, kc, mc * 128: mc * 128 + 128], wmv[:, kc, :], kc == 0, kc == 7, [r_memt, rsl], [rp])
                S.copy("dve", VM[:, mc, :], ps, [rp], [r_vm])
            sl, rsl = ring_next()
            wqm = sl[:, 0:4096].rearrange("p (k n) -> p k n", k=8)
            SC_MEM = float(128 ** -0.5)
            qcnt = 0
            for t in range(4):
                tc_ = slice(t * 512, t * 512 + 512)
                for h in range(4):
                    qb = qcnt % 2
                    qcnt += 1
                    ps, rp = next_ps()
                    for kc in range(8):
                        S.mm(ps, wqm[:, kc, h * 128: h * 128 + 128], HB[:, kc, tc_], kc == 0, kc == 7, [r_hb[t], rsl], [rp])
                    S.copy("act", QM[qb], ps, [rp], [r_qm[qb]])
                    psO, rO = next_acc()
                    psR, rR = next_acc()
                    for mc in range(2):
                        psS, rS_ = next_ps()
                        S.mm(psS, KM[:, h, mc * 128: mc * 128 + 128], QM[qb], True, True, [r_km, r_qm[qb]], [rS_])
                        pt, rpt = next_pt()
                        S.actf(pt, psS, AF.Exp, [rS_], [rpt], scale=SC_MEM)
                        S.mm(psO, VM[:, mc, h * 128: h * 128 + 128], pt, mc == 0, mc == 1, [r_vm, rpt], [rO])
                        S.mm(psR, ONESB, pt, mc == 0, mc == 1, [r_const, rpt], [rR])
                    S.actf(RIM, psR, AF.Ln, [rR], [r_rim])
                    S.actf(RIM, RIM, AF.Exp, [r_rim], [r_rim], scale=-1.0)
                    S.tt("dve", OC[:, h, tc_], psO, RIM, ALU.mult, [rO, r_rim], [r_oc[t]])

            if debug and s == 0 and l == 0:
                rd = S.dres("dbgo")
                for i, (Ox, rOx) in enumerate(((OA, r_oa), (OB, r_ob), (OC, r_oc))):
                    S.dma("sp", dbg_o[i], Ox, rd, reads=list(rOx))
            S.barrier()
            o = P2
            MG = vb(o, 8 * SEQ).rearrange("p (c t) -> p c t", c=8); o += 32768
            SIG = [vf(o + 2048 * i, 512) for i in range(2)]; o += 4096
            ACC = vf(o, 512); o += 2048
            TM = [vf(o + 2048 * i, 512) for i in range(2)]; o += 4096
            assert o <= ARENA_BYTES
            r_mg = [S.res("mg%d" % t) for t in range(4)]
            r_sig = [S.res("sig0"), S.res("sig1")]
            r_acc = S.res("acc")
            r_tm = [S.res("tm0"), S.res("tm1")]
            OBR = [(OA, r_oa), (OB, r_ob), (OC, r_oc)]
            gcnt = 0
            WBR = [vb(o + 3072 * i, 1536).rearrange("p (k n c) -> p k n c", k=4, n=3) for i in range(2)]; o += 6144
            assert o <= ARENA_BYTES
            r_wbr = [S.dres("wbr0"), S.dres("wbr1")]

            def load_wbr(c):
                for n in range(3):
                    S.dma("pool", WBR[c % 2][:, :, n, :], wsrc(w_br[l, n], c * 128, 128), r_wbr[c % 2], writes=[r_wbr[c % 2]])
            load_wbr(0)
            for c in range(8):
                slg, rslg = ring_next()
                wg = slg[:, 0:3072].rearrange("p (k n c) -> p k n c", k=8, n=3)
                if c + 1 < 8:
                    load_wbr(c + 1)
                wb, rslb = WBR[c % 2], r_wbr[c % 2]
                for t in range(4):
                    tc_ = slice(t * 512, t * 512 + 512)
                    for n in range(3):
                        gi = gcnt % 2
                        gcnt += 1
                        psG, rG = next_ps()
                        for kc in range(8):
                            S.mm(psG, wg[:, kc, n, :], HB[:, kc, tc_], kc == 0, kc == 7, [r_hb[t], rslg], [rG])
                        S.actf(SIG[gi], psG, AF.Sigmoid, [rG], [r_sig[gi]])
                        psB, rB = next_ps()
                        On, rOn = OBR[n]
                        for kc in range(4):
                            S.mm(psB, wb[:, kc, n, :], On[:, kc, tc_], kc == 0, kc == 3, [rOn[t], rslb], [rB])
                        if n == 0:
                            S.tt("dve", ACC, SIG[gi], psB, ALU.mult, [r_sig[gi], rB], [r_acc])
                        elif n == 1:
                            S.tt("dve", TM[0], SIG[gi], psB, ALU.mult, [r_sig[gi], rB], [r_tm[0]])
                            S.tt("dve", ACC, ACC, TM[0], ALU.add, [r_acc, r_tm[0]], [r_acc])
                        else:
                            S.tt("dve", TM[1], SIG[gi], psB, ALU.mult, [r_sig[gi], rB], [r_tm[1]])
                            S.tt("dve", MG[:, c, tc_], ACC, TM[1], ALU.add, [r_acc, r_tm[1]], [r_mg[t]])

            if debug and s == 0 and l == 0:
                rd = S.dres("dbgm")
                S.dma("sp", dbg_mg, MG, rd, reads=list(r_mg))
            S.barrier()
            XB3 = [vf(PR + 16384 * i, 4096).rearrange("p (c t) -> p c t", c=8) for i in range(3)]

            def p6_load(gt):
                load_x(XB3[gt % 3], r_xb[gt % 3], R_tile(R0, gt), r_R0[gt])

            def p6_mm(gt):
                X, rX = XB3[gt % 3], r_xb[gt % 3]
                for ch in range(2):
                    sl, rsl = ring_next()
                    wo = sl[:, 0:4096].rearrange("p (k n) -> p k n", k=8)
                    for cc in range(4):
                        c = ch * 4 + cc
                        ps, rp = next_ps()
                        for kc in range(8):
                            S.mm(ps, wo[:, kc, cc * 128: cc * 128 + 128], MG[:, kc, gt * 512: gt * 512 + 512], kc == 0, kc == 7,
                                 [r_mg[gt], rsl], [rp])
                        S.stt("dve", X[:, c, :], X[:, c, :], ALPHA, ps, ALU.mult, ALU.add, [rX[c], rp], [rX[c]])

            def p6_st(gt):
                ln_stats(XB3[gt % 3], r_xb[gt % 3], gt % 2)

            def p6_ln(gt):
                X, rX = XB3[gt % 3], r_xb[gt % 3]
                ln_apply(X, rX, gt % 2, gt, LN1G(l), LN1B(l))
                store_x(X, rX, R_tile(R1, gt), r_R1[gt])
                if debug and s == 0 and l == 0:
                    store_x(X, rX, R_tile(dbg_h1, gt), S.res("dbg"))

            p6_load(0)
            p6_load(1)
            p6_load(2)
            p6_mm(0)
            p6_mm(1)
            p6_st(0)
            p6_st(1)
            p6_ln(0)
            p6_load(3)
            p6_mm(2)
            p6_st(2)
            p6_ln(1)
            p6_mm(3)
            p6_st(3)
            p6_ln(2)
            p6_ln(3)

            S.barrier()
            HID = vb(PR + 32768, 32 * 1024).rearrange("p (c t) -> p c t", c=32)
            assert PR + 32768 + 65536 <= ARENA_BYTES
            r_hid = [S.res("hid0"), S.res("hid1")]
            last = (l == depth - 1)
            for hf in range(2):
                load_xb_half(R1[:, :, hf * 1024:(hf + 1) * 1024], [r_R1[hf * 2], r_R1[hf * 2 + 1]])
                rcnt = 0
                for g in range(8):
                    sl, rsl = ring_next()
                    w1 = sl[:, 0:4096].rearrange("p (k n) -> p k n", k=8)
                    for cc in range(4):
                        for tl in range(2):
                            gt = hf * 2 + tl
                            ps, rp = next_ps()
                            for kc in range(8):
                                S.mm(ps, w1[:, kc, cc * 128: cc * 128 + 128], HB[:, kc, gt * 512: gt * 512 + 512], kc == 0, kc == 7,
                                     [r_hb[gt], rsl], [rp])
                            ti = 2 + (rcnt % 4)
                            rcnt += 1
                            S.actf(TMPF[ti], ps, AF.Relu, [rp], [r_tmp[ti]])
                            S.tt("dve", HID[:, g * 4 + cc, tl * 512: tl * 512 + 512], TMPF[ti], TMPF[ti], ALU.mult,
                                 [r_tmp[ti]], [r_hid[tl]])
                for c in range(8):
                    sl, rsl = ring_next()
                    w2 = sl[:, 0:4096].rearrange("p (k n) -> p k n", k=32)
                    for tl in range(2):
                        ps, rp = next_ps()
                        for kc in range(32):
                            S.mm(ps, w2[:, kc, :], HID[:, kc, tl * 512: tl * 512 + 512], kc == 0, kc == 31, [r_hid[tl], rsl], [rp])
                        x = XB[:, c, tl * 512: tl * 512 + 512]
                        S.stt("dve", x, x, ALPHA, ps, ALU.mult, ALU.add, [r_xb[tl][c], rp], [r_xb[tl][c]])
                if hf == 1 and not last:
                    S.barrier()
                for tl in range(2):
                    ln_stats(XBT(tl), r_xb[tl], tl)
                tail_ops = []
                for tl in range(2):
                    gt = hf * 2 + tl
                    ln_apply(XBT(tl), r_xb[tl], tl, gt, LN2G(l), LN2B(l))
                    if last:
                        tail_ops.append(store_x(XBT(tl), r_xb[tl], os_[:, :, gt * 512: gt * 512 + 512], r_out))
                    else:
                        tail_ops.append(store_x(XBT(tl), r_xb[tl], R_tile(R0, gt), r_R0[gt]))
                prev_tail = tail_ops

    assert ring_state["used"] == len(ring_tasks), (ring_state, len(ring_tasks))
    S.emit()
    return nc


def _host_consts(inputs):
    f = np.float32
    cv = np.zeros((128, 848), f)
    p = np.arange(128)

    def chunked(v):
        return np.asarray(v, f).reshape(-1, 128).T

    cv[:, 0:8] = chunked(inputs["ln_in_g"])
    cv[:, 8:16] = chunked(inputs["ln_in_b"])
    for l in range(DEPTH):
        cv[:, 16 + 8 * l: 24 + 8 * l] = chunked(inputs["ln1_g"][l])
        cv[:, 48 + 8 * l: 56 + 8 * l] = chunked(inputs["ln1_b"][l])
        cv[:, 80 + 8 * l: 88 + 8 * l] = chunked(inputs["ln2_g"][l])
        cv[:, 112 + 8 * l: 120 + 8 * l] = chunked(inputs["ln2_b"][l])
        cv[:, 144 + 3 * l: 147 + 3 * l] = chunked(inputs["g_cq"][l])
        cv[:, 156 + 2 * l: 158 + 2 * l] = chunked(inputs["g_ckv"][l])
        cv[:, 176 + 128 * l: 304 + 128 * l] = np.tile(np.asarray(inputs["b_forget"][l], f), 16)[None, :]
    inv_freq = (10000.0 ** (-np.arange(0, 32, 2, dtype=np.float32) / np.float32(32))).astype(f)
    cv[64:96, 164] = np.concatenate([inv_freq, inv_freq])
    cv[64:96, 165] = np.concatenate([-np.ones(16, f), np.ones(16, f)])
    cv[:, 166] = 1e-5
    cv[:, 167] = 1e-6
    cv[:, 168] = 1.0
    cm = np.zeros((128, 5, 128), f)
    s_ = p[:, None]
    t_ = p[None, :]
    cm[:, 0, :] = 1.0
    cm[:, 1, :] = (s_ <= t_)
    cm[:, 2, :] = (s_ <= 63)
    cm[:, 3, :] = (s_ <= t_)
    cm[:, 4, :] = 1.0 - ((s_ >= 64) & (t_ < 64))
    return cv, cm


_CACHE = {}


def kernel(**inputs):
    inputs = {k: np.asarray(v) for k, v in inputs.items()}
    nseq = 2
    if "nc" not in _CACHE:
        _CACHE["nc"] = build_program(DEPTH, nseq)
    nc = _CACHE["nc"]
    cv, cm = _host_consts(inputs)
    x = inputs["x"]
    mem = inputs["mem"]
    pos = inputs["positions"].astype(np.int32)
    shared = {k: np.ascontiguousarray(inputs[k], dtype=np.float32) for k in
              ("w_in", "w_uq", "w_ukv", "w_mem_kv", "w_br", "w_out", "w_ff1", "w_ff2")}
    in_maps = []
    for c in range(NCORE):
        b0 = c * nseq
        m = dict(shared)
        m["xT"] = np.ascontiguousarray(x[b0:b0 + nseq].transpose(0, 2, 1))
        m["memT"] = np.ascontiguousarray(mem[b0:b0 + nseq].transpose(0, 2, 1))
        m["posr"] = np.ascontiguousarray(np.broadcast_to(pos[b0:b0 + nseq, None, :], (nseq, 32, SEQ)))
        m["cvec"] = cv
        m["cmat"] = cm
        in_maps.append(m)
    res = run_bass_kernel_spmd(nc, in_maps, core_ids=list(range(NCORE)))
    out = np.empty((16, SEQ, D), np.float32)
    for c in range(NCORE):
        o = np.asarray(res.results[c]["outT"])
        out[c * nseq:(c + 1) * nseq] = o.transpose(0, 2, 1)
    return out
```

```python
import numpy as np
import concourse.bass as bass
import concourse.mybir as mybir
from concourse.bass_utils import run_bass_kernel_spmd

F32 = mybir.dt.float32
BF16 = mybir.dt.bfloat16
I32 = mybir.dt.int32
AF = mybir.ActivationFunctionType
ALU = mybir.AluOpType

D = 1024
SEQ = 2048
DEPTH = 4
NCORE = 8
D_IN = 5800
C_FQ, C_FK, C_FV, C_F, C_CQ, C_CKV, C_KR, C_QM, C_G = 0, 512, 1024, 1536, 1544, 1928, 2184, 2216, 2728
ALPHA = float((2 * DEPTH) ** 0.25)
PI = float(np.pi)
TWO_PI = float(2 * np.pi)


class Res:
    __slots__ = ("name", "last_w", "readers", "dsem", "ring")

    def __init__(self, name):
        self.name = name
        self.last_w = None
        self.readers = []
        self.dsem = None
        self.ring = False


class DSem:
    __slots__ = ("sem", "count", "last", "ring")

    def __init__(self, sem):
        self.sem = sem
        self.count = 0
        self.last = None
        self.ring = False


class Op:
    __slots__ = ("eng", "fn", "deps", "idx", "need_inc", "cnt", "dsem", "dcount", "is_dma")

    def __init__(self, eng, fn, idx):
        self.eng = eng
        self.fn = fn
        self.idx = idx
        self.deps = []
        self.need_inc = False
        self.cnt = None
        self.dsem = None
        self.dcount = None
        self.is_dma = False


class Sched:
    ENGS = ("pe", "act", "dve", "pool", "sp")

    def __init__(self, nc):
        self.nc = nc
        self.eng_obj = {"pe": nc.tensor, "act": nc.scalar, "dve": nc.vector,
                        "pool": nc.gpsimd, "sp": nc.sync}
        self.ops = {e: [] for e in self.ENGS}
        self.sems = {}
        self._semctx = []
        for e in self.ENGS:
            self.sems[e] = self._new_sem("s_" + e)
        self.dsems = []

    def _new_sem(self, name):
        ctx = self.nc.semaphore(name)
        s = ctx.__enter__()
        self._semctx.append(ctx)
        return s

    def res(self, name):
        return Res(name)

    def dres(self, name, ring=False):
        r = Res(name)
        r.dsem = DSem(self._new_sem("d_%d" % len(self.dsems)))
        r.dsem.ring = ring
        self.dsems.append(r.dsem)
        return r

    def _add_dep(self, op, dep, raw):
        if dep is None or dep is op:
            return
        if dep.is_dma:
            op.deps.append(dep)
            return
        if dep.eng == op.eng:
            if op.is_dma:
                op.deps.append(dep)
                dep.need_inc = True
            elif raw and op.eng != "pe" and (op.idx - dep.idx) <= 2:
                op.deps.append(dep)
                dep.need_inc = True
            return
        op.deps.append(dep)
        dep.need_inc = True

    def op(self, eng, fn, reads=(), writes=(), dma=None):
        lst = self.ops[eng]
        o = Op(eng, fn, len(lst))
        if dma is not None:
            o.is_dma = True
            o.dsem = dma.dsem
            dma.dsem.count += 16
            o.dcount = dma.dsem.count
            dma.dsem.last = o
        for r in reads:
            self._add_dep(o, r.last_w, True)
        for w in writes:
            self._add_dep(o, w.last_w, False)
            for rd in w.readers:
                self._add_dep(o, rd, False)
        for r in reads:
            r.readers.append(o)
        for w in writes:
            w.last_w = o
            w.readers = []
        lst.append(o)
        return o

    def barrier(self):
        lasts = {}
        for e in self.ENGS:
            for o in reversed(self.ops[e]):
                if not o.is_dma and o.fn is not None:
                    lasts[e] = o
                    break
        dlast = [ds.last for ds in self.dsems if ds.last is not None and not ds.ring]
        for e in self.ENGS:
            lst = self.ops[e]
            o = Op(e, None, len(lst))
            for e2, lo in lasts.items():
                if e2 != e:
                    o.deps.append(lo)
                    lo.need_inc = True
            for d in dlast:
                o.deps.append(d)
            lst.append(o)

    def emit(self):
        for e in self.ENGS:
            c = 0
            for o in self.ops[e]:
                if o.is_dma or o.fn is None:
                    continue
                if o.need_inc:
                    c += 1
                    o.cnt = c
        for e in self.ENGS:
            eng = self.eng_obj[e]
            waited = {}
            for o in self.ops[e]:
                need = {}
                for d in o.deps:
                    if d.is_dma:
                        key, val = d.dsem.sem, d.dcount
                    else:
                        key, val = self.sems[d.eng], d.cnt
                    kid = id(key)
                    if waited.get(kid, 0) >= val:
                        continue
                    if kid not in need or need[kid][1] < val:
                        need[kid] = (key, val)
                for kid, (key, val) in need.items():
                    eng.wait_ge(key, val)
                    waited[kid] = val
                if o.fn is None:
                    continue
                ins = o.fn(eng)
                if o.is_dma:
                    ins.then_inc(o.dsem.sem, 16)
                elif o.need_inc:
                    ins.then_inc(self.sems[e], 1)
        sp = self.eng_obj["sp"]
        for ds in self.dsems:
            if ds.count > 0:
                sp.wait_ge(ds.sem, ds.count)

    def mm(self, out, lhsT, rhs, start, stop, reads, writes):
        return self.op("pe", lambda e: e.matmul(out, lhsT=lhsT, rhs=rhs, start=start, stop=stop), reads, writes)

    def actf(self, out, in_, func, reads, writes, bias=None, scale=1.0):
        if bias is None:
            return self.op("act", lambda e: e.activation(out=out, in_=in_, func=func, scale=scale), reads, writes)
        return self.op("act", lambda e: e.activation(out=out, in_=in_, func=func, bias=bias, scale=scale), reads, writes)

    def tt(self, eng, out, in0, in1, op, reads, writes):
        return self.op(eng, lambda e: e.tensor_tensor(out=out, in0=in0, in1=in1, op=op), reads, writes)

    def ts(self, eng, out, in0, s1, op0, reads, writes, s2=None, op1=None):
        if op1 is None:
            return self.op(eng, lambda e: e.tensor_scalar(out=out, in0=in0, scalar1=s1, scalar2=None, op0=op0), reads, writes)
        return self.op(eng, lambda e: e.tensor_scalar(out=out, in0=in0, scalar1=s1, scalar2=s2, op0=op0, op1=op1), reads, writes)

    def stt(self, eng, out, in0, scalar, in1, op0, op1, reads, writes):
        return self.op(eng, lambda e: e.scalar_tensor_tensor(out=out, in0=in0, scalar=scalar, in1=in1, op0=op0, op1=op1), reads, writes)

    def copy(self, eng, out, in_, reads, writes):
        if eng == "act":
            return self.op("act", lambda e: e.activation(out=out, in_=in_, func=AF.Copy), reads, writes)
        return self.op(eng, lambda e: e.tensor_copy(out=out, in_=in_), reads, writes)

    def recip(self, out, in_, reads, writes):
        return self.op("dve", lambda e: e.reciprocal(out=out, in_=in_), reads, writes)

    def memset(self, eng, ap, val, writes):
        return self.op(eng, lambda e: e.memset(ap, val), (), writes)

    def dma(self, eng, out, in_, track, reads=(), writes=()):
        return self.op(eng, lambda e: e.dma_start(out=out, in_=in_), reads, writes, dma=track)


def build_program(depth=DEPTH, nseq=2, debug=False):
    nc = bass.Bass("TRN2", target_bir_lowering=False)

    def din(name, shape, dt=F32):
        return nc.dram_tensor(name, list(shape), dt, kind="ExternalInput").ap()

    xT = din("xT", [nseq, D, SEQ])
    memT = din("memT", [nseq, D, 256])
    posr = din("posr", [nseq, 32, SEQ], I32)
    w_in = din("w_in", [DEPTH, D, D_IN])
    w_uq = din("w_uq", [DEPTH, 384, 768])
    w_ukv = din("w_ukv", [DEPTH, 256, 1024])
    w_mem_kv = din("w_mem_kv", [DEPTH, D, 1024])
    w_br = din("w_br", [DEPTH, 3, 512, D])
    w_out = din("w_out", [DEPTH, D, D])
    w_ff1 = din("w_ff1", [DEPTH, D, 4096])
    w_ff2 = din("w_ff2", [DEPTH, 4096, D])
    cvec = din("cvec", [128, 848])
    cmat = din("cmat", [128, 5, 128])
    outT = nc.dram_tensor("outT", [nseq, D, SEQ], F32, kind="ExternalOutput").ap()
    R0 = nc.dram_tensor("R0", [128, 8, SEQ], F32, kind="Internal").ap()
    R1 = nc.dram_tensor("R1", [128, 8, SEQ], F32, kind="Internal").ap()
    ROPE_D = nc.dram_tensor("ROPE_D", [32, 2, SEQ], F32, kind="Internal").ap()

    if debug:
        dbg_h0 = nc.dram_tensor("dbg_h0", [128, 8, SEQ], F32, kind="ExternalOutput").ap()
        dbg_h1 = nc.dram_tensor("dbg_h1", [128, 8, SEQ], F32, kind="ExternalOutput").ap()
        dbg_o = [nc.dram_tensor("dbg_o%d" % i, [128, 4, SEQ], BF16, kind="ExternalOutput").ap() for i in range(3)]
        dbg_mg = nc.dram_tensor("dbg_mg", [128, 8, SEQ], BF16, kind="ExternalOutput").ap()
    S = Sched(nc)
    ARENA_BYTES = 212000
    arena = nc.alloc_sbuf_tensor("arena", [128, ARENA_BYTES // 2], BF16)

    def vb(off, n):
        assert off % 4 == 0 and off + 2 * n <= ARENA_BYTES, (off, n)
        return arena[:, off // 2: off // 2 + n]

    def vf(off, n):
        assert off % 4 == 0 and off + 4 * n <= ARENA_BYTES, (off, n)
        return arena[:, off // 2: off // 2 + 2 * n].bitcast(F32)

    def vi(off, n):
        return arena[:, off // 2: off // 2 + 2 * n].bitcast(I32)

    o = 0
    HB = vb(o, 8 * SEQ).rearrange("p (c t) -> p c t", c=8); o += 32768
    CV = vf(o, 848); o += 3392
    CMF = vf(o, 3 * 128).rearrange("p (a b) -> p a b", a=3); o += 1536
    CMB = vb(o, 3 * 128).rearrange("p (a b) -> p a b", a=3); o += 768
    o = 38912
    RING = [vb(o + 8192 * i, 4096) for i in range(4)]; o += 32768
    WUQ = vb(o, 3 * 768).rearrange("p (k n) -> p k n", k=3); o += 4608
    WUQR = vb(o, 3 * 768).rearrange("p (k n) -> p k n", k=3); o += 4608
    WUKV = vb(o, 2 * 1024).rearrange("p (k n) -> p k n", k=2); o += 4096
    WKR = vb(o, 8 * 96).rearrange("p (k n) -> p k n", k=8); o += 1536
    WKRR = vb(o, 8 * 96).rearrange("p (k n) -> p k n", k=8); o += 1536
    PT = [vb(o + 1024 * i, 512) for i in range(6)]; o += 6144
    TMPF = [vf(o + 2048 * i, 512) for i in range(6)]; o += 12288
    XSQ = [vb(o + 1024 * i, 512) for i in range(4)]; o += 4096
    PR = o
    assert PR <= 110592, PR

    LNIN_G, LNIN_B = CV[:, 0:8], CV[:, 8:16]
    def LN1G(l): return CV[:, 16 + 8 * l: 24 + 8 * l]
    def LN1B(l): return CV[:, 48 + 8 * l: 56 + 8 * l]
    def LN2G(l): return CV[:, 80 + 8 * l: 88 + 8 * l]
    def LN2B(l): return CV[:, 112 + 8 * l: 120 + 8 * l]
    def GCQ(l): return CV[:, 144 + 3 * l: 147 + 3 * l]
    def GCKV(l): return CV[:, 156 + 2 * l: 158 + 2 * l]
    INVF = CV[:, 164:165]
    SGN = CV[:, 165:166]
    EPS5 = CV[:, 166:167]
    EPS6 = CV[:, 167:168]
    ONEC = CV[:, 168:169]
    ZEROC = CV[:, 169:170]
    def BFB(l): return CV[:, 176 + 128 * l: 304 + 128 * l]
    ONES_F, TRI_F, HALF_F = CMF[:, 0, :], CMF[:, 1, :], CMF[:, 2, :]
    ONESB, TRIMASK, MLAMASK = CMB[:, 0, :], CMB[:, 1, :], CMB[:, 2, :]

    r_hb = [S.res("hb%d" % t) for t in range(4)]
    r_const = S.dres("const")
    r_ring = [S.dres("ring%d" % i, ring=True) for i in range(4)]
    r_mlaw = S.dres("mlaw")
    r_pt = [S.res("pt%d" % i) for i in range(6)]
    r_tmp = [S.res("tmp%d" % i) for i in range(6)]
    r_xsq = [S.res("xsq%d" % i) for i in range(4)]
    r_ps = [S.res("ps%d" % i) for i in range(8)]
    PS = [nc.alloc_psum_tensor("ps%d" % i, [128, 512], F32)[:, :] for i in range(8)]
    r_xb = [[S.dres("xb%d_%d" % (t, c)) for c in range(8)] for t in range(3)]
    r_R0 = [S.res("R0_%d" % t) for t in range(4)]
    r_R1 = [S.res("R1_%d" % t) for t in range(4)]
    r_out = S.res("out")
    r_roped = S.res("roped")

    cnt = {"ps": 0, "pt": 0, "xsq": 0, "ring": 0, "acc": 0}

    def next_ps():
        i = cnt["ps"] % 5
        cnt["ps"] += 1
        return PS[i], r_ps[i]

    def next_acc():
        i = 5 + cnt["acc"] % 3
        cnt["acc"] += 1
        return PS[i], r_ps[i]

    def next_pt():
        i = cnt["pt"] % 6
        cnt["pt"] += 1
        return PT[i], r_pt[i]

    def next_xsq():
        i = cnt["xsq"] % 4
        cnt["xsq"] += 1
        return XSQ[i], r_xsq[i]

    ring_tasks = []
    ring_state = {"issued": 0, "used": 0}

    def ring_issue_upto(k):
        while ring_state["issued"] < min(k + 1, len(ring_tasks)):
            i = ring_state["issued"]
            slot = i % 4
            for dst, src in ring_tasks[i](RING[slot]):
                S.dma("pool", dst, src, r_ring[slot], writes=[r_ring[slot]])
            ring_state["issued"] += 1

    def ring_next():
        i = ring_state["used"]
        ring_issue_upto(i + 3)
        ring_state["used"] += 1
        slot = i % 4
        return RING[slot], r_ring[slot]

    def wsrc(w2d, c0, ncol):
        return w2d.rearrange("(k p) n -> p k n", p=128)[:, :, c0:c0 + ncol]

    for s in range(nseq):
        for l in range(depth):
            wl = w_in[l]
            for hp in range(4):
                def f(sl, wl=wl, hp=hp):
                    v = sl[:, 0:3072].rearrange("p (k n c) -> p k n c", k=8, n=3)
                    r = [(v[:, :, 0, :], wsrc(wl, C_FQ + hp * 128, 128)),
                         (v[:, :, 1, :], wsrc(wl, C_FK + hp * 128, 128)),
                         (v[:, :, 2, :], wsrc(wl, C_FV + hp * 128, 128))]
                    if hp == 0:
                        r.append((sl[:, 3072:3136].rearrange("p (k n) -> p k n", k=8), wsrc(wl, C_F, 8)))
                    return r
                ring_tasks.append(f)
            ring_tasks.append(lambda sl, wl=wl: [(sl[:, 0:2048].rearrange("p (k n) -> p k n", k=8), wsrc(wl, C_CKV, 256))])
            ring_tasks.append(lambda sl, wl=wl: [(sl[:, 0:3072].rearrange("p (k n) -> p k n", k=8), wsrc(wl, C_CQ, 384))])
            ring_tasks.append(lambda sl, l=l: [(sl[:, 0:4096].rearrange("p (k n) -> p k n", k=8), wsrc(w_mem_kv[l], 0, 512))])
            ring_tasks.append(lambda sl, l=l: [(sl[:, 0:4096].rearrange("p (k n) -> p k n", k=8), wsrc(w_mem_kv[l], 512, 512))])
            ring_tasks.append(lambda sl, wl=wl: [(sl[:, 0:4096].rearrange("p (k n) -> p k n", k=8), wsrc(wl, C_QM, 512))])
            for c in range(8):
                def fg(sl, wl=wl, c=c):
                    v = sl[:, 0:3072].rearrange("p (k n c) -> p k n c", k=8, n=3)
                    return [(v[:, :, n, :], wsrc(wl, C_G + n * 1024 + c * 128, 128)) for n in range(3)]
                ring_tasks.append(fg)
            for gt in range(4):
                for ch in range(2):
                    ring_tasks.append(lambda sl, l=l, ch=ch: [(sl[:, 0:4096].rearrange("p (k n) -> p k n", k=8), wsrc(w_out[l], ch * 512, 512))])
            for hf in range(2):
                for g in range(8):
                    ring_tasks.append(lambda sl, l=l, g=g: [(sl[:, 0:4096].rearrange("p (k n) -> p k n", k=8), wsrc(w_ff1[l], g * 512, 512))])
                for c in range(8):
                    ring_tasks.append(lambda sl, l=l, c=c: [(sl[:, 0:4096].rearrange("p (k n) -> p k n", k=32), wsrc(w_ff2[l], c * 128, 128))])

    S.dma("sp", CV, cvec, r_const, writes=[r_const])
    S.dma("sp", CMF, cmat[:, 0:3, :], r_const, writes=[r_const])
    S.dma("pool", CMB[:, 0, :], cmat[:, 0, :], r_const, writes=[r_const])
    S.dma("pool", CMB[:, 1:3, :], cmat[:, 3:5, :], r_const, writes=[r_const])
    S.memset("pool", WUQR, 0.0, [r_mlaw])
    S.memset("pool", WKR, 0.0, [r_mlaw])
    S.memset("pool", WKRR, 0.0, [r_mlaw])

    XB = vf(PR, 8 * 1024).rearrange("p (c t) -> p c t", c=8)

    def ln_stats(X, rX, k):
        psM, rM = next_acc()
        psV, rV = next_acc()
        for c in range(8):
            xb, rxb = next_xsq()
            S.copy("act", xb, X[:, c, :], [rX[c]], [rxb])
            sq, rsq = next_xsq()
            S.actf(sq, X[:, c, :], AF.Square, [rX[c]], [rsq])
            S.mm(psM, ONESB, xb, c == 0, c == 7, [rxb, r_const], [rM])
            S.mm(psV, ONESB, sq, c == 0, c == 7, [rsq, r_const], [rV])
        MEAN, RSTD = TMPF[2 * k], TMPF[2 * k + 1]
        rmean, rrstd = r_tmp[2 * k], r_tmp[2 * k + 1]
        S.actf(MEAN, psM, AF.Copy, [rM], [rmean], scale=1.0 / 1024)
        S.tt("dve", RSTD, MEAN, MEAN, ALU.mult, [rmean], [rrstd])
        S.stt("dve", RSTD, psV, 1.0 / 1024, RSTD, ALU.mult, ALU.subtract, [rV, rrstd], [rrstd])
        S.actf(RSTD, RSTD, AF.Ln, [rrstd, r_const], [rrstd], bias=EPS5)
        S.actf(RSTD, RSTD, AF.Exp, [rrstd], [rrstd], scale=-0.5)

    def ln_apply(X, rX, k, gtile, G, Bv):
        gcols = slice(gtile * 512, gtile * 512 + 512)
        MEAN, RSTD = TMPF[2 * k], TMPF[2 * k + 1]
        rmean, rrstd = r_tmp[2 * k], r_tmp[2 * k + 1]
        for c in range(8):
            x = X[:, c, :]
            rx = rX[c]
            S.tt("dve", x, x, MEAN, ALU.subtract, [rx, rmean], [rx])
            S.stt("dve", x, x, G[:, c:c + 1], RSTD, ALU.mult, ALU.mult, [rx, rrstd, r_const], [rx])
            S.actf(HB[:, c, gcols], x, AF.Identity, [rx, r_const], [r_hb[gtile]], bias=Bv[:, c:c + 1])
            S.actf(x, x, AF.Identity, [rx, r_const], [rx], bias=Bv[:, c:c + 1])

    def layer_norm_x(X, rX, gtile, G, Bv):
        ln_stats(X, rX, 0)
        ln_apply(X, rX, 0, gtile, G, Bv)

    def XBT(tl):
        return XB[:, :, tl * 512: tl * 512 + 512]

    def layer_norm_tile(tl, gtile, G, Bv):
        layer_norm_x(XBT(tl), r_xb[tl], gtile, G, Bv)

    def store_x(X, rX, dst, rdst):
        return S.dma("sp", dst, X, rX[0], reads=list(rX), writes=[rdst])

    def load_x(X, rX, src, rsrc):
        S.dma("sp", X[:, 0:4, :], src[:, 0:4, :], rX[0], reads=[rsrc], writes=list(rX[0:4]))
        S.dma("sp", X[:, 4:8, :], src[:, 4:8, :], rX[4], reads=[rsrc], writes=list(rX[4:8]))

    def store_xb_tile(tl, dst, rdst):
        store_x(XBT(tl), r_xb[tl], dst, rdst)

    def load_xb_half(src, rsrcs):
        for tl in range(2):
            load_x(XBT(tl), r_xb[tl], src[:, :, tl * 512: tl * 512 + 512], rsrcs[tl])

    def run_units(nunits, make_unit, on_start, L=3):
        units = {}
        njs = {}
        steps = []
        live = {}
        total = None
        g = 0
        pending = []
        ui_next = 0
        while True:
            while len(pending) <= g and ui_next < nunits:
                u = make_unit(ui_next)
                units[ui_next] = u
                for j in range(u[0]):
                    pending.append((ui_next, j))
                ui_next += 1
            if g < len(pending):
                ui, j = pending[g]
                if j == 0:
                    on_start(ui)
                live[g] = units[ui][1](j)
            if g >= L and g - L < len(pending):
                ui, j = pending[g - L]
                units[ui][2](j, *live.pop(g - L))
                if j == units[ui][0] - 1:
                    units[ui][3]()
            g += 1
            if ui_next >= nunits and g - L >= len(pending):
                break

    def pipelined(nj, s_fn, pv_fn, L=3):
        live = {}
        for step in range(nj + L):
            if step < nj:
                live[step] = s_fn(step)
            if step >= L:
                pv_fn(step - L, *live.pop(step - L))

    def R_tile(R, gt):
        return R[:, :, gt * 512: gt * 512 + 512]

    for s in range(nseq):
        xs = xT[s].rearrange("(c p) t -> p c t", p=128)
        os_ = outT[s].rearrange("(c p) t -> p c t", p=128)
        r_x = S.res("xin")
        S.barrier()
        if True:
            o_alias = PR + 49152
            COS = vf(PR + 32768, SEQ)
            SINP = vf(PR + 32768 + 8192, SEQ)
            PI_ = vi(o_alias, SEQ)
            ANG = vf(o_alias + 8192, SEQ)
            KFL = vf(o_alias + 16384, SEQ)
            r_rope = S.dres("rope")
            r_rt = S.res("ropetmp")
            rp_ = slice(64, 96)
            S.dma("sp", PI_[rp_, :], posr[s], r_rope, writes=[r_rt])
            S.copy("dve", ANG[rp_, :], PI_[rp_, :], [r_rt], [r_rt])
            S.ts("dve", ANG[rp_, :], ANG[rp_, :], INVF[rp_, :], ALU.mult, [r_rt, r_const], [r_rt])
            for tab, shift in ((SINP, 0.0), (COS, PI / 2)):
                A2 = tab
                S.ts("dve", A2[rp_, :], ANG[rp_, :], shift, ALU.add, [r_rt], [r_rope])
                S.ts("dve", KFL[rp_, :], A2[rp_, :], 1.0 / TWO_PI, ALU.mult, [r_rope], [r_rt])
                S.copy("dve", PI_[rp_, :], KFL[rp_, :], [r_rt], [r_rt])
                S.copy("dve", KFL[rp_, :], PI_[rp_, :], [r_rt], [r_rt])
                S.stt("dve", A2[rp_, :], KFL[rp_, :], -TWO_PI, A2[rp_, :], ALU.mult, ALU.add, [r_rt, r_rope], [r_rope])
                S.ts("dve", KFL[rp_, :], A2[rp_, :], PI, ALU.is_gt, [r_rope], [r_rt], s2=-TWO_PI, op1=ALU.mult)
                S.tt("dve", A2[rp_, :], A2[rp_, :], KFL[rp_, :], ALU.add, [r_rope, r_rt], [r_rope])
                S.ts("dve", KFL[rp_, :], A2[rp_, :], -PI, ALU.is_lt, [r_rope], [r_rt], s2=TWO_PI, op1=ALU.mult)
                S.tt("dve", A2[rp_, :], A2[rp_, :], KFL[rp_, :], ALU.add, [r_rope, r_rt], [r_rope])
                S.actf(A2[rp_, :], A2[rp_, :], AF.Sin, [r_rope], [r_rope])
            S.ts("dve", SINP[rp_, :], SINP[rp_, :], SGN[rp_, :], ALU.mult, [r_rope, r_const], [r_rope])
            S.dma("sp", ROPE_D[:, 0, :], COS[rp_, :], r_rope, reads=[r_rope], writes=[r_roped])
            S.dma("sp", ROPE_D[:, 1, :], SINP[rp_, :], r_rope, reads=[r_rope], writes=[r_roped])
        S.barrier()
        for hf in range(2):
            load_xb_half(xs[:, :, hf * 1024:(hf + 1) * 1024], [r_x, r_x])
            for tl in range(2):
                ln_stats(XBT(tl), r_xb[tl], tl)
            for tl in range(2):
                gt = hf * 2 + tl
                ln_apply(XBT(tl), r_xb[tl], tl, gt, LNIN_G, LNIN_B)
                store_xb_tile(tl, R_tile(R0, gt), r_R0[gt])
                if debug and s == 0:
                    store_xb_tile(tl, R_tile(dbg_h0, gt), S.res("dbg"))

        for l in range(depth):
            wl = w_in[l]
            if l == 0:
                S.barrier()
                prev_tail = []
            S.dma("pool", WUQ, w_uq[l].rearrange("(k p) n -> p k n", p=128), r_mlaw, writes=[r_mlaw])
            S.dma("pool", WUKV, w_ukv[l].rearrange("(k p) n -> p k n", p=128), r_mlaw, writes=[r_mlaw])
            uq4 = w_uq[l].rearrange("(k p) (h c) -> p k h c", p=128, c=96)
            wr4 = WUQR.rearrange("p k (h c) -> p k h c", c=96)
            for k in range(3):
                S.dma("pool", wr4[:, k, :, 64:80], uq4[:, k, :, 80:96], r_mlaw, writes=[r_mlaw])
                S.dma("pool", wr4[:, k, :, 80:96], uq4[:, k, :, 64:80], r_mlaw, writes=[r_mlaw])
            S.dma("pool", WKR[:, :, 64:96], wsrc(wl, C_KR, 32), r_mlaw, writes=[r_mlaw])
            S.dma("pool", WKRR[:, :, 64:80], wsrc(wl, C_KR + 16, 16), r_mlaw, writes=[r_mlaw])
            S.dma("pool", WKRR[:, :, 80:96], wsrc(wl, C_KR, 16), r_mlaw, writes=[r_mlaw])

            OA = vb(PR, 4 * SEQ).rearrange("p (c t) -> p c t", c=4)
            OB = vb(PR + 16384, 4 * SEQ).rearrange("p (c t) -> p c t", c=4)
            OC = vb(PR + 32768, 4 * SEQ).rearrange("p (c t) -> p c t", c=4)
            r_oa = [S.res("oa%d" % t) for t in range(4)]
            for t in range(4):
                r_oa[t].readers.extend(prev_tail)
            r_ob = [S.res("ob%d" % t) for t in range(4)]
            r_oc = [S.res("oc%d" % t) for t in range(4)]
            P2 = PR + 49152

            o = P2
            BIAS = vf(o, 136 * 8); o += 4608
            GT = vf(o, 128); o += 512
            PRE = vf(o, 128); o += 512
            CC = vf(o, 128); o += 512
            CMID = vf(o, 128); o += 512
            KT = [vb(o + 4096 * i, SEQ) for i in range(2)]; o += 8192
            VA = [vb(o + 8192 * i, 16 * 256).rearrange("p (t c) -> p t c", t=16) for i in range(2)]; o += 16384
            QTZ = [[vb(o + 2048 * i + 1024 * e, 512) for e in range(2)] for i in range(2)]; o += 4096
            RS2 = [vf(o + 2048 * i, 512) for i in range(2)]; o += 4096
            assert o <= ARENA_BYTES
            r_bias = [S.res("bias%d" % t) for t in range(4)]; r_g = S.res("g"); r_pre = S.res("pre"); r_cc = S.res("cc")
            r_kt = [S.res("kt0"), S.res("kt1")]
            r_va = [S.res("va0"), S.res("va1")]
            r_qt = [[S.res("qt%d%d" % (i, e)) for e in range(2)] for i in range(2)]
            r_rs = [S.res("rs0"), S.res("rs1")]
            for i in range(2):
                va4 = VA[i].rearrange("p t (a c) -> p t a c", a=4)
                S.memset("pool", va4[:, :, 1, :], 1.0, [r_va[i]])
                S.memset("pool", va4[:, :, 3, :], 1.0, [r_va[i]])
                S.memset("pool", QTZ[i][0][64:128, :], 0.0, [r_qt[i][0]])
                S.memset("pool", QTZ[i][1][0:64, :], 0.0, [r_qt[i][1]])

            fox_state = {}

            def fox_prep_pair(hp):
                sl, rsl = ring_next()
                wv3 = sl[:, 0:3072].rearrange("p (k n c) -> p k n c", k=8, n=3)
                fox_state[hp] = (wv3, rsl)
                if hp == 0:
                    wf = sl[:, 3072:3136].rearrange("p (k n) -> p k n", k=8)
                    psF, rF = next_acc()
                    for t16 in range(16):
                        for kc in range(8):
                            S.mm(psF[:, t16 * 8: t16 * 8 + 8], HB[:, kc, t16 * 128: t16 * 128 + 128], wf[:, kc, :],
                                 kc == 0, kc == 7, [r_hb[t16 // 4], rsl], [rF])
                    S.tt("dve", GT, psF[:, 0:128], BFB(l), ALU.add, [rF, r_const], [r_g])
                    S.actf(GT, GT, AF.Exp, [r_g], [r_g], scale=-1.0)
                    S.actf(GT, GT, AF.Ln, [r_g, r_const], [r_g], bias=ONEC)
                b = hp % 2
                for t in range(4):
                    ps, rp = next_ps()
                    for kc in range(8):
                        S.mm(ps, wv3[:, kc, 1, :], HB[:, kc, t * 512: t * 512 + 512], kc == 0, kc == 7, [r_hb[t], rsl], [rp])
                    S.copy("act", KT[b][:, t * 512: t * 512 + 512], ps, [rp], [r_kt[b]])
                va4 = VA[b].rearrange("p t (a c) -> p t a c", a=4)
                for g4 in range(4):
                    ps, rp = next_ps()
                    for tt_ in range(4):
                        t16 = g4 * 4 + tt_
                        for kc in range(8):
                            S.mm(ps[:, tt_ * 128: tt_ * 128 + 128], HB[:, kc, t16 * 128: t16 * 128 + 128], wv3[:, kc, 2, :],
                                 kc == 0, kc == 7, [r_hb[g4], rsl], [rp])
                    ps3 = ps.rearrange("p (t c) -> p t c", t=4)
                    S.copy("dve", va4[:, g4 * 4: g4 * 4 + 4, 0, :], ps3[:, :, 0:64], [rp], [r_va[b]])
                    S.copy("dve", va4[:, g4 * 4: g4 * 4 + 4, 2, :], ps3[:, :, 64:128], [rp], [r_va[b]])

                if hp == 0:
                    psT, rT = next_acc()
                    psW, rW = next_acc()
                    psMd, rMd = next_acc()
                    S.mm(psT[:, 0:128], ONES_F, GT, True, True, [r_g, r_const], [rT])
                    S.mm(psW[:, 0:128], TRI_F, GT, True, True, [r_g, r_const], [rW])
                    S.mm(psMd[:, 0:128], HALF_F, GT, True, True, [r_g, r_const], [rMd])
                    S.memset("dve", PRE[:, 0:8], 0.0, [r_pre])
                    for j in range(1, 16):
                        S.tt("dve", PRE[:, j * 8: j * 8 + 8], PRE[:, j * 8 - 8: j * 8], psT[:, j * 8 - 8: j * 8], ALU.add,
                             [r_pre, rT], [r_pre])
                    S.tt("dve", CC, PRE, psW[:, 0:128], ALU.add, [r_pre, rW], [r_cc])
                    S.tt("dve", CMID, PRE, psMd[:, 0:128], ALU.add, [r_pre, rMd], [r_cc])
                    for i in range(16):
                        for j in range(i + 1):
                            idx = i * (i + 1) // 2 + j
                            S.tt("pool", BIAS[:, idx * 8: idx * 8 + 8], CC[:, j * 8: j * 8 + 8], CMID[:, i * 8: i * 8 + 8],
                                 ALU.subtract, [r_cc], [r_bias[i // 4]])

            def fox_prep_q(hp, t, qb):
                wv3, rsl = fox_state[hp]
                ps, rp = next_ps()
                for kc in range(8):
                    S.mm(ps, wv3[:, kc, 0, :], HB[:, kc, t * 512: t * 512 + 512], kc == 0, kc == 7, [r_hb[t], rsl], [rp])
                S.copy("act", QTZ[qb][0][0:64, :], ps[0:64, :], [rp], [r_qt[qb][0]])
                S.copy("act", QTZ[qb][1][64:128, :], ps[64:128, :], [rp], [r_qt[qb][1]])

            def fox_unit(hp, t, e, qb, ri):
                b = hp % 2
                h = 2 * hp + e
                prt = slice(e * 64, e * 64 + 64)
                nj = 4 * t + 4
                st = {}

                def s_fn(j):
                    if j == 0:
                        st["psO"], st["rO"] = next_acc()
                    d = j - 4 * t
                    c0 = max(0, d) * 128
                    psS, rS_ = next_ps()
                    S.mm(psS[:, c0:512], KT[b][:, j * 128: j * 128 + 128], QTZ[qb][e][:, c0:512], True, True,
                         [r_kt[b], r_qt[qb][e]], [rS_])
                    pt, rpt = next_pt()
                    for qi in range(c0 // 128, 4):
                        i = 4 * t + qi
                        idx = i * (i + 1) // 2 + j
                        S.actf(pt[:, qi * 128: qi * 128 + 128], psS[:, qi * 128: qi * 128 + 128], AF.Exp,
                               [rS_, r_bias[t]], [rpt], bias=BIAS[:, idx * 8 + h: idx * 8 + h + 1], scale=0.125)
                    if d >= 0:
                        S.tt("dve", pt[:, c0:c0 + 128], pt[:, c0:c0 + 128], TRIMASK, ALU.mult, [rpt, r_const], [rpt])
                    return pt, rpt, c0

                def pv_fn(j, pt, rpt, c0):
                    S.mm(st["psO"][:, c0:512], VA[b][:, j, e * 128: e * 128 + 128], pt[:, c0:512], j == 0, j == nj - 1,
                         [r_va[b], rpt], [st["rO"]])

                def fin_fn():
                    psO, rO = st["psO"], st["rO"]
                    RSx, rrs = RS2[ri], r_rs[ri]
                    S.recip(RSx[0:64, :], psO[64:128, :], [rO], [rrs])
                    S.tt("dve", OA[prt, hp, t * 512: t * 512 + 512], psO[0:64, :], RSx[0:64, :], ALU.mult,
                         [rO, rrs], [r_oa[t]])
                return nj, s_fn, pv_fn, fin_fn

            pairs = [(hp, t) for hp in range(4) for t in range(4)]
            fox_prep_pair(0)
            fox_prep_q(0, 0, 0)

            def fox_make(ui):
                hp, t = pairs[ui // 2]
                return fox_unit(hp, t, ui % 2, (ui // 2) % 2, ui % 2)

            def fox_start(ui):
                if ui % 2 == 1 and ui // 2 + 1 < len(pairs):
                    hp, t = pairs[ui // 2]
                    nhp, nt = pairs[ui // 2 + 1]
                    if nhp != hp:
                        fox_prep_pair(nhp)
                    fox_prep_q(nhp, nt, (ui // 2 + 1) % 2)
            run_units(2 * len(pairs), fox_make, fox_start)

            S.barrier()
            o = P2
            COS = vf(PR + 32768, SEQ)
            SINP = vf(PR + 32768 + 8192, SEQ)
            CKV = vb(o, 2 * SEQ).rearrange("p (c t) -> p c t", c=2); o += 8192
            KPE = vb(o, SEQ); o += 4096
            RAW = [vf(o + 2048 * i, 512) for i in range(3)]; o += 6144
            RINV = vf(o, 512); o += 2048
            QF = [vb(o + 1024 * i, 512) for i in range(2)]; o += 2048
            o_alias = o
            CQ = vb(o, 3 * SEQ).rearrange("p (c t) -> p c t", c=3); o += 12288
            KF = [vb(o + 4096 * i, SEQ) for i in range(2)]; o += 8192
            VAH = [vb(o + 4096 * i, 16 * 128).rearrange("p (t c) -> p t c", t=16) for i in range(2)]; o += 8192
            assert o <= ARENA_BYTES, o
            r_rope = S.dres("rope")
            rp_ = slice(64, 96)
            S.dma("sp", COS[rp_, :], ROPE_D[:, 0, :], r_rope, reads=[r_roped], writes=[r_rope])
            S.dma("sp", SINP[rp_, :], ROPE_D[:, 1, :], r_rope, reads=[r_roped], writes=[r_rope])

            r_ckv = [S.res("ckv%d" % t) for t in range(4)]
            r_cq = [S.res("cq%d" % t) for t in range(4)]
            r_kpe = S.res("kpe")
            r_raw = [S.res("raw%d" % i) for i in range(3)]
            r_rinv = S.res("rinv")
            r_rs = S.res("rs2")
            r_qf = [S.res("qf0"), S.res("qf1")]
            r_kf = [S.res("kf0"), S.res("kf1")]
            r_vah = [S.res("vah0"), S.res("vah1")]
            for i in range(2):
                S.memset("pool", VAH[i][:, :, 64:128], 1.0, [r_vah[i]])

            def latent(sl, rsl, nch, dst, rdst, gfn, t, inv_n):
                psN, rN = next_acc()
                for cch in range(nch):
                    ps, rp = next_ps()
                    for kc in range(8):
                        S.mm(ps, sl[:, kc, cch * 128: cch * 128 + 128], HB[:, kc, t * 512: t * 512 + 512], kc == 0, kc == 7,
                             [r_hb[t], rsl], [rp])
                    S.copy("act", RAW[cch], ps, [rp], [r_raw[cch]])
                    sq, rsq = next_xsq()
                    S.actf(sq, ps, AF.Square, [rp], [rsq])
                    S.mm(psN, ONESB, sq, cch == 0, cch == nch - 1, [rsq, r_const], [rN])
                S.actf(RINV, psN, AF.Ln, [rN, r_const], [r_rinv], bias=EPS6, scale=inv_n)
                S.actf(RINV, RINV, AF.Exp, [r_rinv], [r_rinv], scale=-0.5)
                for cch in range(nch):
                    S.stt("dve", dst[:, cch, t * 512: t * 512 + 512], RAW[cch], gfn[:, cch:cch + 1], RINV, ALU.mult, ALU.mult,
                          [r_raw[cch], r_rinv, r_const], [rdst[t]])

            sl, rsl = ring_next()
            wckv = sl[:, 0:2048].rearrange("p (k n) -> p k n", k=8)
            for t in range(4):
                latent(wckv, rsl, 2, CKV, r_ckv, GCKV(l), t, 1.0 / 256)
                psA, rA = next_ps()
                psB, rB = next_ps()
                for kc in range(8):
                    S.mm(psA[0:96, :], WKR[:, kc, :], HB[:, kc, t * 512: t * 512 + 512], kc == 0, kc == 7, [r_hb[t], r_mlaw], [rA])
                for kc in range(8):
                    S.mm(psB[0:96, :], WKRR[:, kc, :], HB[:, kc, t * 512: t * 512 + 512], kc == 0, kc == 7, [r_hb[t], r_mlaw], [rB])
                tc_ = slice(t * 512, t * 512 + 512)
                S.tt("dve", TMPF[2][rp_, :], psA[rp_, :], COS[rp_, tc_], ALU.mult, [rA, r_rope], [r_tmp[2]])
                S.tt("dve", TMPF[3][rp_, :], psB[rp_, :], SINP[rp_, tc_], ALU.mult, [rB, r_rope], [r_tmp[3]])
                S.tt("pool", KPE[rp_, tc_], TMPF[2][rp_, :], TMPF[3][rp_, :], ALU.add, [r_tmp[2], r_tmp[3]], [r_kpe])
            sl, rsl = ring_next()
            wcq = sl[:, 0:3072].rearrange("p (k n) -> p k n", k=8)
            for t in range(4):
                latent(wcq, rsl, 3, CQ, r_cq, GCQ(l), t, 1.0 / 384)

            SC_MLA = float(96 ** -0.5)
            for i in range(2):
                S.memset("pool", KF[i], 0.0, [r_kf[i]])
                S.memset("pool", QF[i], 0.0, [r_qf[i]])
            RS2 = [RAW[1], RAW[2]]
            r_rs2 = [r_raw[1], r_raw[2]]

            def mla_prep_head(h):
                b = h % 2
                for t in range(4):
                    ps, rp = next_ps()
                    for kc in range(2):
                        S.mm(ps[0:64, :], WUKV[:, kc, h * 128: h * 128 + 64], CKV[:, kc, t * 512: t * 512 + 512], kc == 0, kc == 1,
                             [r_ckv[t], r_mlaw], [rp])
                    S.copy("act", KF[b][0:64, t * 512: t * 512 + 512], ps[0:64, :], [rp], [r_kf[b]])
                S.copy("pool", KF[b][rp_, :], KPE[rp_, :], [r_kpe], [r_kf[b]])
                for g8 in range(2):
                    ps, rp = next_ps()
                    for tt_ in range(8):
                        t16 = g8 * 8 + tt_
                        for kc in range(2):
                            S.mm(ps[:, tt_ * 64: tt_ * 64 + 64], CKV[:, kc, t16 * 128: t16 * 128 + 128],
                                 WUKV[:, kc, h * 128 + 64: h * 128 + 128], kc == 0, kc == 1, [r_ckv[t16 // 4], r_mlaw], [rp])
                    S.copy("dve", VAH[b][:, g8 * 8: g8 * 8 + 8, 0:64], ps.rearrange("p (t c) -> p t c", t=8), [rp], [r_vah[b]])

            def mla_prep_q(h, t, qb):
                tc_ = slice(t * 512, t * 512 + 512)
                psA, rA = next_ps()
                psB, rB = next_ps()
                for kc in range(3):
                    S.mm(psA[0:96, :], WUQ[:, kc, h * 96: h * 96 + 96], CQ[:, kc, tc_], kc == 0, kc == 2, [r_cq[t], r_mlaw], [rA])
                for kc in range(3):
                    S.mm(psB[0:96, :], WUQR[:, kc, h * 96: h * 96 + 96], CQ[:, kc, tc_], kc == 0, kc == 2, [r_cq[t], r_mlaw], [rB])
                S.copy("act", QF[qb][0:64, :], psA[0:64, :], [rA], [r_qf[qb]])
                ta, tb = (2, 3) if qb == 0 else (4, 5)
                S.tt("dve", TMPF[ta][rp_, :], psA[rp_, :], COS[rp_, tc_], ALU.mult, [rA, r_rope], [r_tmp[ta]])
                S.tt("dve", TMPF[tb][rp_, :], psB[rp_, :], SINP[rp_, tc_], ALU.mult, [rB, r_rope], [r_tmp[tb]])
                S.tt("pool", QF[qb][rp_, :], TMPF[ta][rp_, :], TMPF[tb][rp_, :], ALU.add, [r_tmp[ta], r_tmp[tb]], [r_qf[qb]])

            def mla_unit(h, t, qb):
                b = h % 2
                tc_ = slice(t * 512, t * 512 + 512)
                nj = 4 * t + 4
                st = {}

                def s_fn(j):
                    if j == 0:
                        st["psO"], st["rO"] = next_acc()
                    d = j - 4 * t
                    c0 = max(0, d) * 128
                    psS, rS_ = next_ps()
                    S.mm(psS[:, c0:512], KF[b][:, j * 128: j * 128 + 128], QF[qb][:, c0:512], True, True,
                         [r_kf[b], r_qf[qb]], [rS_])
                    pt, rpt = next_pt()
                    S.actf(pt[:, c0:512], psS[:, c0:512], AF.Exp, [rS_], [rpt], scale=SC_MLA)
                    if d >= 0:
                        S.tt("dve", pt[:, c0:c0 + 128], pt[:, c0:c0 + 128], MLAMASK, ALU.mult, [rpt, r_const], [rpt])
                    return pt, rpt, c0

                def pv_fn(j, pt, rpt, c0):
                    S.mm(st["psO"][:, c0:512], VAH[b][:, j, :], pt[:, c0:512], j == 0, j == nj - 1, [r_vah[b], rpt], [st["rO"]])

                def fin_fn():
                    psO, rO = st["psO"], st["rO"]
                    RSx, rrs = RS2[qb], r_rs2[qb]
                    S.actf(RSx[0:64, :], psO[64:128, :], AF.Ln, [rO], [rrs])
                    S.actf(RSx[0:64, :], RSx[0:64, :], AF.Exp, [rrs], [rrs], scale=-1.0)
                    S.tt("dve", OB[(h % 2) * 64:(h % 2) * 64 + 64, h // 2, tc_], psO[0:64, :], RSx[0:64, :], ALU.mult,
                         [rO, rrs], [r_ob[t]])
                return nj, s_fn, pv_fn, fin_fn

            munits = [(h, t) for h in range(8) for t in range(4)]
            mla_prep_head(0)
            mla_prep_q(0, 0, 0)

            def mla_make(ui):
                h, t = munits[ui]
                return mla_unit(h, t, ui % 2)

            def mla_start(ui):
                if ui + 1 < len(munits):
                    h, t = munits[ui]
                    nh, nt = munits[ui + 1]
                    if nh != h:
                        mla_prep_head(nh)
                    mla_prep_q(nh, nt, (ui + 1) % 2)
            run_units(len(munits), mla_make, mla_start)

            S.barrier()
            o = P2
            MEMT = vb(o, 8 * 256).rearrange("p (k m) -> p k m", k=8); o += 4096
            KM = vb(o, 4 * 256).rearrange("p (h m) -> p h m", h=4); o += 2048
            VM = vb(o, 2 * 512).rearrange("p (c n) -> p c n", c=2); o += 2048
            QM = [vb(o + 1024 * i, 512) for i in range(2)]; o += 2048
            RIM = vf(o, 512); o += 2048
            r_memt = S.dres("memt")
            r_km = S.res("km"); r_vm = S.res("vm")
            r_qm = [S.res("qm0"), S.res("qm1")]
            r_rim = S.res("rim")
            S.dma("pool", MEMT, memT[s].rearrange("(k p) m -> p k m", p=128), r_memt, writes=[r_memt])
            sl, rsl = ring_next()
            wmk = sl[:, 0:4096].rearrange("p (k n) -> p k n", k=8)
            for h in range(4):
                ps, rp = next_ps()
                for kc in range(8):
                    S.mm(ps[:, 0:256], wmk[:, kc, h * 128: h * 128 + 128], MEMT[:, kc, :], kc == 0, kc == 7, [r_memt, rsl], [rp])
                S.copy("act", KM[:, h, :], ps[:, 0:256], [rp], [r_km])
            sl, rsl = ring_next()
            wmv = sl[:, 0:4096].rearrange("p (k n) -> p k n", k=8)
            for mc in range(2):
                ps, rp = next_ps()
                for kc in range(8):
                    S.mm(ps, MEMT[:, kc, mc * 128: mc * 128 + 128], wmv[:, kc, :], kc == 0, kc == 7, [r_memt, rsl], [rp])
                S.copy("dve", VM[:, mc, :], ps, [rp], [r_vm])
            sl, rsl = ring_next()
            wqm = sl[:, 0:4096].rearrange("p (k n) -> p k n", k=8)
            SC_MEM = float(128 ** -0.5)
            qcnt = 0
            for t in range(4):
                tc_ = slice(t * 512, t * 512 + 512)
                for h in range(4):
                    qb = qcnt % 2
                    qcnt += 1
                    ps, rp = next_ps()
                    for kc in range(8):
                        S.mm(ps, wqm[:, kc, h * 128: h * 128 + 128], HB[:, kc, tc_], kc == 0, kc == 7, [r_hb[t], rsl], [rp])
                    S.copy("act", QM[qb], ps, [rp], [r_qm[qb]])
                    psO, rO = next_acc()
                    psR, rR = next_acc()
                    for mc in range(2):
                        psS, rS_ = next_ps()
                        S.mm(psS, KM[:, h, mc * 128: mc * 128 + 128], QM[qb], True, True, [r_km, r_qm[qb]], [rS_])
                        pt, rpt = next_pt()
                        S.actf(pt, psS, AF.Exp, [rS_], [rpt], scale=SC_MEM)
                        S.mm(psO, VM[:, mc, h * 128: h * 128 + 128], pt, mc == 0, mc == 1, [r_vm, rpt], [rO])
                        S.mm(psR, ONESB, pt, mc == 0, mc == 1, [r_const, rpt], [rR])
                    S.actf(RIM, psR, AF.Ln, [rR], [r_rim])
                    S.actf(RIM, RIM, AF.Exp, [r_rim], [r_rim], scale=-1.0)
                    S.tt("dve", OC[:, h, tc_], psO, RIM, ALU.mult, [rO, r_rim], [r_oc[t]])

            if debug and s == 0 and l == 0:
                rd = S.dres("dbgo")
                for i, (Ox, rOx) in enumerate(((OA, r_oa), (OB, r_ob), (OC, r_oc))):
                    S.dma("sp", dbg_o[i], Ox, rd, reads=list(rOx))
            S.barrier()
            o = P2
            MG = vb(o, 8 * SEQ).rearrange("p (c t) -> p c t", c=8); o += 32768
            SIG = [vf(o + 2048 * i, 512) for i in range(2)]; o += 4096
            ACC = vf(o, 512); o += 2048
            TM = [vf(o + 2048 * i, 512) for i in range(2)]; o += 4096
            assert o <= ARENA_BYTES
            r_mg = [S.res("mg%d" % t) for t in range(4)]
            r_sig = [S.res("sig0"), S.res("sig1")]
            r_acc = S.res("acc")
            r_tm = [S.res("tm0"), S.res("tm1")]
            OBR = [(OA, r_oa), (OB, r_ob), (OC, r_oc)]
            gcnt = 0
            WBR = [vb(o + 3072 * i, 1536).rearrange("p (k n c) -> p k n c", k=4, n=3) for i in range(2)]; o += 6144
            assert o <= ARENA_BYTES
            r_wbr = [S.dres("wbr0"), S.dres("wbr1")]

            def load_wbr(c):
                for n in range(3):
                    S.dma("pool", WBR[c % 2][:, :, n, :], wsrc(w_br[l, n], c * 128, 128), r_wbr[c % 2], writes=[r_wbr[c % 2]])
            load_wbr(0)
            for c in range(8):
                slg, rslg = ring_next()
                wg = slg[:, 0:3072].rearrange("p (k n c) -> p k n c", k=8, n=3)
                if c + 1 < 8:
                    load_wbr(c + 1)
                wb, rslb = WBR[c % 2], r_wbr[c % 2]
                for t in range(4):
                    tc_ = slice(t * 512, t * 512 + 512)
                    for n in range(3):
                        gi = gcnt % 2
                        gcnt += 1
                        psG, rG = next_ps()
                        for kc in range(8):
                            S.mm(psG, wg[:, kc, n, :], HB[:, kc, tc_], kc == 0, kc == 7, [r_hb[t], rslg], [rG])
                        S.actf(SIG[gi], psG, AF.Sigmoid, [rG], [r_sig[gi]])
                        psB, rB = next_ps()
                        On, rOn = OBR[n]
                        for kc in range(4):
                            S.mm(psB, wb[:, kc, n, :], On[:, kc, tc_], kc == 0, kc == 3, [rOn[t], rslb], [rB])
                        if n == 0:
                            S.tt("dve", ACC, SIG[gi], psB, ALU.mult, [r_sig[gi], rB], [r_acc])
                        elif n == 1:
                            S.tt("dve", TM[0], SIG[gi], psB, ALU.mult, [r_sig[gi], rB], [r_tm[0]])
                            S.tt("dve", ACC, ACC, TM[0], ALU.add, [r_acc, r_tm[0]], [r_acc])
                        else:
                            S.tt("dve", TM[1], SIG[gi], psB, ALU.mult, [r_sig[gi], rB], [r_tm[1]])
                            S.tt("dve", MG[:, c, tc_], ACC, TM[1], ALU.add, [r_acc, r_tm[1]], [r_mg[t]])

            if debug and s == 0 and l == 0:
                rd = S.dres("dbgm")
                S.dma("sp", dbg_mg, MG, rd, reads=list(r_mg))
            S.barrier()
            XB3 = [vf(PR + 16384 * i, 4096).rearrange("p (c t) -> p c t", c=8) for i in range(3)]

            def p6_load(gt):
                load_x(XB3[gt % 3], r_xb[gt % 3], R_tile(R0, gt), r_R0[gt])

            def p6_mm(gt):
                X, rX = XB3[gt % 3], r_xb[gt % 3]
                for ch in range(2):
                    sl, rsl = ring_next()
                    wo = sl[:, 0:4096].rearrange("p (k n) -> p k n", k=8)
                    for cc in range(4):
                        c = ch * 4 + cc
                        ps, rp = next_ps()
                        for kc in range(8):
                            S.mm(ps, wo[:, kc, cc * 128: cc * 128 + 128], MG[:, kc, gt * 512: gt * 512 + 512], kc == 0, kc == 7,
                                 [r_mg[gt], rsl], [rp])
                        S.stt("dve", X[:, c, :], X[:, c, :], ALPHA, ps, ALU.mult, ALU.add, [rX[c], rp], [rX[c]])

            def p6_st(gt):
                ln_stats(XB3[gt % 3], r_xb[gt % 3], gt % 2)

            def p6_ln(gt):
                X, rX = XB3[gt % 3], r_xb[gt % 3]
                ln_apply(X, rX, gt % 2, gt, LN1G(l), LN1B(l))
                store_x(X, rX, R_tile(R1, gt), r_R1[gt])
                if debug and s == 0 and l == 0:
                    store_x(X, rX, R_tile(dbg_h1, gt), S.res("dbg"))

            p6_load(0)
            p6_load(1)
            p6_load(2)
            p6_mm(0)
            p6_mm(1)
            p6_st(0)
            p6_st(1)
            p6_ln(0)
            p6_load(3)
            p6_mm(2)
            p6_st(2)
            p6_ln(1)
            p6_mm(3)
            p6_st(3)
            p6_ln(2)
            p6_ln(3)

            S.barrier()
            HID = vb(PR + 32768, 32 * 1024).rearrange("p (c t) -> p c t", c=32)
            assert PR + 32768 + 65536 <= ARENA_BYTES
            r_hid = [S.res("hid0"), S.res("hid1")]
            last = (l == depth - 1)
            for hf in range(2):
                load_xb_half(R1[:, :, hf * 1024:(hf + 1) * 1024], [r_R1[hf * 2], r_R1[hf * 2 + 1]])
                rcnt = 0
                for g in range(8):
                    sl, rsl = ring_next()
                    w1 = sl[:, 0:4096].rearrange("p (k n) -> p k n", k=8)
                    for cc in range(4):
                        for tl in range(2):
                            gt = hf * 2 + tl
                            ps, rp = next_ps()
                            for kc in range(8):
                                S.mm(ps, w1[:, kc, cc * 128: cc * 128 + 128], HB[:, kc, gt * 512: gt * 512 + 512], kc == 0, kc == 7,
                                     [r_hb[gt], rsl], [rp])
                            ti = 2 + (rcnt % 4)
                            rcnt += 1
                            S.actf(TMPF[ti], ps, AF.Relu, [rp], [r_tmp[ti]])
                            S.tt("dve", HID[:, g * 4 + cc, tl * 512: tl * 512 + 512], TMPF[ti], TMPF[ti], ALU.mult,
                                 [r_tmp[ti]], [r_hid[tl]])
                for c in range(8):
                    sl, rsl = ring_next()
                    w2 = sl[:, 0:4096].rearrange("p (k n) -> p k n", k=32)
                    for tl in range(2):
                        ps, rp = next_ps()
                        for kc in range(32):
                            S.mm(ps, w2[:, kc, :], HID[:, kc, tl * 512: tl * 512 + 512], kc == 0, kc == 31, [r_hid[tl], rsl], [rp])
                        x = XB[:, c, tl * 512: tl * 512 + 512]
                        S.stt("dve", x, x, ALPHA, ps, ALU.mult, ALU.add, [r_xb[tl][c], rp], [r_xb[tl][c]])
                if hf == 1 and not last:
                    S.barrier()
                for tl in range(2):
                    ln_stats(XBT(tl), r_xb[tl], tl)
                tail_ops = []
                for tl in range(2):
                    gt = hf * 2 + tl
                    ln_apply(XBT(tl), r_xb[tl], tl, gt, LN2G(l), LN2B(l))
                    if last:
                        tail_ops.append(store_x(XBT(tl), r_xb[tl], os_[:, :, gt * 512: gt * 512 + 512], r_out))
                    else:
                        tail_ops.append(store_x(XBT(tl), r_xb[tl], R_tile(R0, gt), r_R0[gt]))
                prev_tail = tail_ops

    assert ring_state["used"] == len(ring_tasks), (ring_state, len(ring_tasks))
    S.emit()
    return nc


def _host_consts(inputs):
    f = np.float32
    cv = np.zeros((128, 848), f)
    p = np.arange(128)

    def chunked(v):
        return np.asarray(v, f).reshape(-1, 128).T

    cv[:, 0:8] = chunked(inputs["ln_in_g"])
    cv[:, 8:16] = chunked(inputs["ln_in_b"])
    for l in range(DEPTH):
        cv[:, 16 + 8 * l: 24 + 8 * l] = chunked(inputs["ln1_g"][l])
        cv[:, 48 + 8 * l: 56 + 8 * l] = chunked(inputs["ln1_b"][l])
        cv[:, 80 + 8 * l: 88 + 8 * l] = chunked(inputs["ln2_g"][l])
        cv[:, 112 + 8 * l: 120 + 8 * l] = chunked(inputs["ln2_b"][l])
        cv[:, 144 + 3 * l: 147 + 3 * l] = chunked(inputs["g_cq"][l])
        cv[:, 156 + 2 * l: 158 + 2 * l] = chunked(inputs["g_ckv"][l])
        cv[:, 176 + 128 * l: 304 + 128 * l] = np.tile(np.asarray(inputs["b_forget"][l], f), 16)[None, :]
    inv_freq = (10000.0 ** (-np.arange(0, 32, 2, dtype=np.float32) / np.float32(32))).astype(f)
    cv[64:96, 164] = np.concatenate([inv_freq, inv_freq])
    cv[64:96, 165] = np.concatenate([-np.ones(16, f), np.ones(16, f)])
    cv[:, 166] = 1e-5
    cv[:, 167] = 1e-6
    cv[:, 168] = 1.0
    cm = np.zeros((128, 5, 128), f)
    s_ = p[:, None]
    t_ = p[None, :]
    cm[:, 0, :] = 1.0
    cm[:, 1, :] = (s_ <= t_)
    cm[:, 2, :] = (s_ <= 63)
    cm[:, 3, :] = (s_ <= t_)
    cm[:, 4, :] = 1.0 - ((s_ >= 64) & (t_ < 64))
    return cv, cm


_CACHE = {}


def kernel(**inputs):
    inputs = {k: np.asarray(v) for k, v in inputs.items()}
    nseq = 2
    if "nc" not in _CACHE:
        _CACHE["nc"] = build_program(DEPTH, nseq)
    nc = _CACHE["nc"]
    cv, cm = _host_consts(inputs)
    x = inputs["x"]
    mem = inputs["mem"]
    pos = inputs["positions"].astype(np.int32)
    shared = {k: np.ascontiguousarray(inputs[k], dtype=np.float32) for k in
              ("w_in", "w_uq", "w_ukv", "w_mem_kv", "w_br", "w_out", "w_ff1", "w_ff2")}
    in_maps = []
    for c in range(NCORE):
        b0 = c * nseq
        m = dict(shared)
        m["xT"] = np.ascontiguousarray(x[b0:b0 + nseq].transpose(0, 2, 1))
        m["memT"] = np.ascontiguousarray(mem[b0:b0 + nseq].transpose(0, 2, 1))
        m["posr"] = np.ascontiguousarray(np.broadcast_to(pos[b0:b0 + nseq, None, :], (nseq, 32, SEQ)))
        m["cvec"] = cv
        m["cmat"] = cm
        in_maps.append(m)
    res = run_bass_kernel_spmd(nc, in_maps, core_ids=list(range(NCORE)))
    out = np.empty((16, SEQ, D), np.float32)
    for c in range(NCORE):
        o = np.asarray(res.results[c]["outT"])
        out[c * nseq:(c + 1) * nseq] = o.transpose(0, 2, 1)
    return out
```
